# Optimizing a Trainium2 kernel written in Bass

```python
import jax
import jax.numpy as jnp
from jax import lax
import numpy as np

D_MODEL = 1024
BATCH = 8
SEQ = 2048
DEPTH = 1

GRID_W = 64
CTX_LEN = 256
MLSTM_HEADS = 4
MLSTM_DH = 128
MLSTM_W = MLSTM_HEADS * MLSTM_DH
MLSTM_CHUNK = 64
QK_CONV = 3
N_GATES = 4 * MLSTM_HEADS
ATTN_HEADS = 8
KV_HEADS = 2
ATTN_DH = 64
ATTN_W = ATTN_HEADS * ATTN_DH
KV_W = KV_HEADS * ATTN_DH
GQA_GROUP = ATTN_HEADS // KV_HEADS
WINDOW = 128
ATTN_BLOCK = 128
ROPE_BASE = 10000.0
ROPE_AXIS_PAIRS = ATTN_DH // 4
MIX_W = MLSTM_W + ATTN_W
IN_SPLITS = (2 * MLSTM_W, 3 * MLSTM_W, 4 * MLSTM_W, 4 * MLSTM_W + N_GATES,
             4 * MLSTM_W + N_GATES + ATTN_W, 4 * MLSTM_W + N_GATES + ATTN_W + KV_W)
IN_COLS = 4 * MLSTM_W + N_GATES + ATTN_W + 2 * KV_W
D_FF = 2816
FFN_CONV = 3
EPS = 1e-6

kernel_name = 'hymba_mlstm_swa_convglu_dit'


def rmsnorm(x, g):
    xf = x.astype(jnp.float32)
    y = xf * lax.rsqrt(jnp.mean(xf * xf, axis=-1, keepdims=True) + EPS)
    return (y * g.astype(jnp.float32)).astype(x.dtype)


def modulate(h, shift, scale):
    return h * (1 + scale) + shift


def dwconv(x, w, b):
    k = w.shape[0]
    y = lax.conv_general_dilated(x, w[:, None, :], window_strides=(1,), padding=[(k // 2, k // 2)],
                                 dimension_numbers=('NWC', 'WIO', 'NWC'),
                                 feature_group_count=x.shape[-1])
    return y + b


def heads_first(t, n_heads):
    b, l, _ = t.shape
    return t.reshape(b, l, n_heads, -1).transpose(0, 2, 1, 3)


def axial_rope(n_tokens):
    rows = n_tokens // GRID_W
    r, col = jnp.meshgrid(jnp.arange(rows, dtype=jnp.float32), jnp.arange(GRID_W, dtype=jnp.float32),
                          indexing='ij')
    inv = ROPE_BASE ** (-jnp.arange(ROPE_AXIS_PAIRS, dtype=jnp.float32) / ROPE_AXIS_PAIRS)
    ang = jnp.stack([r.reshape(-1)[:, None] * inv, col.reshape(-1)[:, None] * inv], axis=1)
    return jnp.cos(ang), jnp.sin(ang)


def apply_rope(x, cos, sin):
    b, l, h, _ = x.shape
    xr = x.astype(jnp.float32).reshape(b, l, h, 2, 2, ROPE_AXIS_PAIRS)
    x0, x1 = xr[..., 0, :], xr[..., 1, :]
    cs, sn = cos[:, None], sin[:, None]
    y = jnp.stack([x0 * cs - x1 * sn, x0 * sn + x1 * cs], axis=-2)
    return y.reshape(b, l, h, ATTN_DH).astype(x.dtype)


def project_tokens(h, w_in, qk_conv_w, qk_conv_b, gate_b):
    b, l, _ = h.shape
    u = h @ w_in
    qk, v_m, o_m, gates, q_a, k_a, v_a = jnp.split(u, IN_SPLITS, axis=-1)
    q_m, k_m = jnp.split(jax.nn.silu(dwconv(qk, qk_conv_w, qk_conv_b)), 2, axis=-1)
    q_m = heads_first(q_m, MLSTM_HEADS).astype(jnp.float32)
    k_m = heads_first(k_m, MLSTM_HEADS).astype(jnp.float32) * (MLSTM_DH ** -0.5)
    v_m = heads_first(v_m, MLSTM_HEADS).astype(jnp.float32)
    g = (gates + gate_b).astype(jnp.float32).reshape(b, l, 4, MLSTM_HEADS).transpose(2, 0, 3, 1)
    gates = (g[0], jax.nn.log_sigmoid(g[1]), g[2], jax.nn.log_sigmoid(g[3]))
    q_a = q_a.reshape(b, l, ATTN_HEADS, ATTN_DH)
    k_a = k_a.reshape(b, l, KV_HEADS, ATTN_DH)
    v_a = v_a.reshape(b, l, KV_HEADS, ATTN_DH)
    return q_m, k_m, v_m, o_m, gates, q_a, k_a, v_a


def init_state(b):
    return (jnp.zeros((b, MLSTM_HEADS, MLSTM_DH, MLSTM_DH), jnp.float32),
            jnp.zeros((b, MLSTM_HEADS, MLSTM_DH), jnp.float32),
            jnp.zeros((b, MLSTM_HEADS), jnp.float32))


def mlstm_scan(q, k, v, log_i, log_f, init, return_h):
    b, h, l, dh = k.shape
    nc = l // MLSTM_CHUNK
    chunk = lambda t: t.reshape((b, h, nc, MLSTM_CHUNK) + t.shape[3:])
    k, v, li, lf = chunk(k), chunk(v), chunk(log_i), chunk(log_f)
    cum = jnp.cumsum(lf, axis=-1)
    cum_end = cum[..., -1]
    g = cum_end[..., None] - cum + li
    m_loc = jnp.max(g, axis=-1)
    w = jnp.exp(g - m_loc[..., None])
    c_loc = jnp.einsum('bhnsd,bhnse->bhnde', w[..., None] * k, v)
    n_loc = jnp.einsum('bhns,bhnsd->bhnd', w, k)

    def step(carry, inp):
        c_st, n_st, m_st = carry
        c_l, n_l, m_l, bt = inp
        m_new = jnp.maximum(bt + m_st, m_l)
        a = jnp.exp(bt + m_st - m_new)
        s = jnp.exp(m_l - m_new)
        return (a[..., None, None] * c_st + s[..., None, None] * c_l,
                a[..., None] * n_st + s[..., None] * n_l, m_new), (c_st, n_st, m_st)

    final, starts = lax.scan(step, init, tuple(jnp.moveaxis(t, 2, 0) for t in (c_loc, n_loc, m_loc, cum_end)))
    if not return_h:
        return None, final
    c0, n0, m0 = (jnp.moveaxis(t, 0, 2) for t in starts)
    q = chunk(q)
    a_log = cum + m0[..., None]
    d_log = cum[..., :, None] - cum[..., None, :] + li[..., None, :]
    order = jnp.tril(jnp.ones((MLSTM_CHUNK, MLSTM_CHUNK), dtype=bool))
    d_log = jnp.where(order, d_log, -jnp.inf)
    m_t = jnp.maximum(a_log, jnp.max(d_log, axis=-1))
    dec = jnp.exp(d_log - m_t[..., None])
    inter = jnp.exp(a_log - m_t)
    s = jnp.einsum('bhntd,bhnsd->bhnts', q, k) * dec
    num = jnp.einsum('bhnts,bhnse->bhnte', s, v) + inter[..., None] * jnp.einsum('bhntd,bhnde->bhnte', q, c0)
    den = jnp.sum(s, axis=-1) + inter * jnp.einsum('bhntd,bhnd->bhnt', q, n0)
    hid = num / jnp.maximum(jnp.abs(den), jnp.exp(-m_t))[..., None]
    return hid.reshape(b, h, l, dh), final


def mlstm_bidirectional(q, k, v, gates, init_f, init_b, return_h):
    li_f, lf_f, li_b, lf_b = gates
    flip = lambda t: jnp.flip(t, axis=2)
    h_f, st_f = mlstm_scan(q, k, v, li_f, lf_f, init_f, return_h)
    h_b, st_b = mlstm_scan(flip(q), flip(k), flip(v), flip(li_b), flip(lf_b), init_b, return_h)
    hid = h_f + flip(h_b) if return_h else None
    return hid, st_f, st_b


def mlstm_merge(h, o, gain):
    b, _, l, _ = h.shape
    hn = h * lax.rsqrt(jnp.mean(h * h, axis=-1, keepdims=True) + EPS)
    hn = hn.transpose(0, 2, 1, 3).reshape(b, l, MLSTM_W) * gain.astype(jnp.float32)
    return (hn * jax.nn.sigmoid(o.astype(jnp.float32))).astype(o.dtype)


def window_attention(q, k, v, k_ctx, v_ctx, sink):
    b, l = q.shape[:2]
    nb = l // ATTN_BLOCK
    qb = q.reshape(b, nb, ATTN_BLOCK, KV_HEADS, GQA_GROUP, ATTN_DH)

    def band(t):
        tb = jnp.pad(t.reshape(b, nb, ATTN_BLOCK, KV_HEADS, ATTN_DH), ((0, 0), (1, 1), (0, 0), (0, 0), (0, 0)))
        return jnp.concatenate([tb[:, :-2], tb[:, 1:-1], tb[:, 2:]], axis=2)

    kb, vb = band(k), band(v)
    n_band = 3 * ATTN_BLOCK
    qpos = jnp.arange(l).reshape(nb, ATTN_BLOCK)
    kpos = qpos[:, :1] - ATTN_BLOCK + jnp.arange(n_band)
    valid = ((jnp.abs(kpos[:, None, :] - qpos[:, :, None]) <= WINDOW)
             & (kpos >= 0)[:, None, :] & (kpos < l)[:, None, :])
    s_band = jnp.einsum('bnqhgd,bnkhd->bnhgqk', qb, kb).astype(jnp.float32)
    s_band = jnp.where(valid[None, :, None, None], s_band, -jnp.inf)
    s_ctx = jnp.einsum('bnqhgd,bchd->bnhgqc', qb, k_ctx).astype(jnp.float32)
    s_sink = jnp.broadcast_to(sink.astype(jnp.float32).reshape(KV_HEADS, GQA_GROUP, 1, 1),
                              s_band.shape[:-1] + (1,))
    p = jax.nn.softmax(jnp.concatenate([s_band, s_ctx, s_sink], axis=-1), axis=-1).astype(v.dtype)
    out = (jnp.einsum('bnhgqk,bnkhd->bnqhgd', p[..., :n_band], vb)
           + jnp.einsum('bnhgqc,bchd->bnqhgd', p[..., n_band:n_band + k_ctx.shape[1]], v_ctx))
    return out.reshape(b, l, ATTN_W)


def context_attention(q, k, v, sink):
    b, n = q.shape[:2]
    qg = q.reshape(b, n, KV_HEADS, GQA_GROUP, ATTN_DH)
    s = jnp.einsum('bqhgd,bkhd->bhgqk', qg, k).astype(jnp.float32)
    s_sink = jnp.broadcast_to(sink.astype(jnp.float32).reshape(KV_HEADS, GQA_GROUP, 1, 1), s.shape[:-1] + (1,))
    p = jax.nn.softmax(jnp.concatenate([s, s_sink], axis=-1), axis=-1)[..., :-1].astype(v.dtype)
    return jnp.einsum('bhgqk,bkhd->bqhgd', p, v).reshape(b, n, ATTN_W)


def conv_glu(h, w_up, conv_w, conv_b, w_down):
    a, val = jnp.split(h @ w_up, 2, axis=-1)
    a = jax.nn.gelu(dwconv(a, conv_w, conv_b))
    return (a * val) @ w_down


def setup_inputs(seed: int = 0) -> dict:
    key = jax.random.key(seed)
    ks = jax.random.split(key, 20)
    nrm = lambda k, shape, s: jax.random.normal(k, shape, jnp.float32) * s
    f_bias = jnp.linspace(3.0, 6.0, MLSTM_HEADS, dtype=jnp.float32)
    zero_h = jnp.zeros((MLSTM_HEADS,), jnp.float32)
    gate_base = jnp.stack([zero_h, f_bias, zero_h, f_bias]).reshape(-1)
    return {
        'x': nrm(ks[0], (BATCH, SEQ, D_MODEL), 1.0),
        'c': nrm(ks[1], (BATCH, D_MODEL), 1.0),
        'ctx': nrm(ks[2], (BATCH, CTX_LEN, D_MODEL), 1.0),
        'c_ctx': nrm(ks[3], (D_MODEL,), 1.0),
        'w_ada': nrm(ks[4], (DEPTH, D_MODEL, 6 * D_MODEL), 0.5 * D_MODEL ** -0.5),
        'b_ada': nrm(ks[5], (DEPTH, 6 * D_MODEL), 0.02),
        'norm_mix': 1.0 + nrm(ks[6], (DEPTH, D_MODEL), 0.02),
        'norm_ffn': 1.0 + nrm(ks[7], (DEPTH, D_MODEL), 0.02),
        'w_in': nrm(ks[8], (DEPTH, D_MODEL, IN_COLS), D_MODEL ** -0.5),
        'gate_b': gate_base[None, :] + nrm(ks[9], (DEPTH, N_GATES), 0.1),
        'qk_conv_w': nrm(ks[10], (DEPTH, QK_CONV, 2 * MLSTM_W), QK_CONV ** -0.5),
        'qk_conv_b': nrm(ks[11], (DEPTH, 2 * MLSTM_W), 0.02),
        'mlstm_norm': 1.0 + nrm(ks[12], (DEPTH, MLSTM_W), 0.02),
        'attn_sink': nrm(ks[13], (DEPTH, ATTN_HEADS), 0.5),
        'w_out': nrm(ks[14], (DEPTH, MIX_W, D_MODEL), MIX_W ** -0.5),
        'w_up': nrm(ks[15], (DEPTH, D_MODEL, 2 * D_FF), D_MODEL ** -0.5),
        'ffn_conv_w': nrm(ks[16], (DEPTH, FFN_CONV, D_FF), FFN_CONV ** -0.5),
        'ffn_conv_b': nrm(ks[17], (DEPTH, D_FF), 0.02),
        'w_down': nrm(ks[18], (DEPTH, D_FF, D_MODEL), D_FF ** -0.5),
        'final_norm': 1.0 + nrm(ks[19], (D_MODEL,), 0.02),
    }


def reference(x, c, ctx, c_ctx, w_ada, b_ada, norm_mix, norm_ffn, w_in, gate_b, qk_conv_w, qk_conv_b,
              mlstm_norm, attn_sink, w_out, w_up, ffn_conv_w, ffn_conv_b, w_down, final_norm):
    cos, sin = axial_rope(x.shape[1])
    attn_scale = ATTN_DH ** -0.5
    for l in range(DEPTH):
        last = l == DEPTH - 1
        sh_a, sc_a, g_a, sh_f, sc_f, g_f = jnp.split(
            (jax.nn.silu(c) @ w_ada[l] + b_ada[l])[:, None, :], 6, axis=-1)
        csh_a, csc_a, cg_a, csh_f, csc_f, cg_f = jnp.split(
            jax.nn.silu(c_ctx) @ w_ada[l] + b_ada[l], 6, axis=-1)

        hc = modulate(rmsnorm(ctx, norm_mix[l]), csh_a, csc_a)
        qm_c, km_c, vm_c, om_c, gates_c, qa_c, ka_c, va_c = project_tokens(
            hc, w_in[l], qk_conv_w[l], qk_conv_b[l], gate_b[l])
        zero = init_state(ctx.shape[0])
        hm_c, st_f, st_b = mlstm_bidirectional(qm_c, km_c, vm_c, gates_c, zero, zero, not last)

        hx = modulate(rmsnorm(x, norm_mix[l]), sh_a, sc_a)
        qm, km, vm, om, gates_x, qa, ka, va = project_tokens(hx, w_in[l], qk_conv_w[l], qk_conv_b[l], gate_b[l])
        hm, _, _ = mlstm_bidirectional(qm, km, vm, gates_x, st_f, st_b, True)
        y_m = mlstm_merge(hm, om, mlstm_norm[l])
        y_a = window_attention(apply_rope(qa, cos, sin) * attn_scale, apply_rope(ka, cos, sin), va,
                               ka_c, va_c, attn_sink[l])
        x = x + g_a * (jnp.concatenate([y_m, y_a], axis=-1) @ w_out[l])
        x = x + g_f * conv_glu(modulate(rmsnorm(x, norm_ffn[l]), sh_f, sc_f),
                               w_up[l], ffn_conv_w[l], ffn_conv_b[l], w_down[l])

        if not last:
            y_mc = mlstm_merge(hm_c, om_c, mlstm_norm[l])
            y_ac = context_attention(qa_c * attn_scale, ka_c, va_c, attn_sink[l])
            ctx = ctx + cg_a * (jnp.concatenate([y_mc, y_ac], axis=-1) @ w_out[l])
            ctx = ctx + cg_f * conv_glu(modulate(rmsnorm(ctx, norm_ffn[l]), csh_f, csc_f),
                                        w_up[l], ffn_conv_w[l], ffn_conv_b[l], w_down[l])
    return rmsnorm(x, final_norm)
```

```python
from contextlib import ExitStack
import numpy as np
import concourse.bass as bass
import concourse.mybir as mybir
from concourse.bass_utils import run_bass_kernel_spmd

F32 = mybir.dt.float32
BF16 = mybir.dt.bfloat16
U8 = mybir.dt.uint8
AF = mybir.ActivationFunctionType
ALU = mybir.AluOpType
AX = mybir.AxisListType

ENGS = ("pe", "act", "dve", "pool", "sp")
DT_SIZE = {F32: 4, BF16: 2}

import os as _os
F_SILU = _os.environ.get("K_SILU", "1") == "1"
F_ARSTD = _os.environ.get("K_ARSTD", "1") == "1"
F_INTER = _os.environ.get("K_INTER", "1") == "1"

D = 1024
L = 2048
NT = 16
CTXL = 256
NCT = 2
NTT = NT + NCT
KC = 8
DFF = 2816
NCG = 22
INC = 2832
EPS = 1e-6
KAPPA = 1.0 / np.sqrt(128.0)
LNK = float(np.log(KAPPA if F_SILU else KAPPA * 0.25))
NEG = -30000.0
ARENA = 207 * 1024


def _name(k):
    return k[0] if isinstance(k, tuple) else k


class KB:
    def __init__(self, n_dma_slots=16):
        self.nc = bass.Bass("TRN2", target_bir_lowering=False)
        self.es = ExitStack()
        self.ops = {e: [] for e in ENGS}
        self.seq = {e: 0 for e in ENGS}
        self.sems = {}
        for e in ENGS:
            self.sems[e] = self.es.enter_context(self.nc.semaphore("s_" + e))
        self.nslots = {"sp": n_dma_slots, "act": 2, "pool": n_dma_slots}
        self.dma_slot_next = {}
        self.dma_slot_val = {}
        for q in ("sp", "act", "pool"):
            for i in range(self.nslots[q]):
                key = ("dma", q, i)
                self.sems[key] = self.es.enter_context(self.nc.semaphore("d_%s%d" % (q, i)))
                self.dma_slot_val[key] = 0
            self.dma_slot_next[q] = 0
        self.last_w = {}
        self.readers = {}
        self.waited = {e: {} for e in ENGS}
        self.keys_by_name = {}
        self.arena = self.nc.alloc_sbuf_tensor("arena", [128, ARENA], U8)
        self.abase = self.nc.lookup_mloc(self.arena).addr
        self.atop = 0
        self.ahigh = 0
        self.live = []
        self.alias = {}
        self.alias_sum = {}
        self.nbank = 0
        self.dumps = []

    def sb(self, name, shape, dt, key=None, at=None):
        key = key or name
        size = int(np.prod(shape[1:])) * DT_SIZE[dt]
        size = (size + 63) // 64 * 64
        if at is not None:
            start = at
        else:
            start = self.atop
            self.atop += size
            self.ahigh = max(self.ahigh, self.atop)
            assert self.atop <= ARENA, (name, self.atop)
        old = [n for (n, s, e) in self.live if s < start + size and e > start and n != key]
        if old:
            self.alias.setdefault(key, set()).update(old)
        self.live.append((key, start, start + size))
        return self.nc.alloc_sbuf_tensor_at(name, list(shape), dt, offset=self.abase + start)

    def mark(self):
        return self.atop

    def reset(self, m):
        self.atop = m

    def ps(self, name, shape, dt):
        return self.es.enter_context(self.nc.psum_tensor(name, list(shape), dt))

    def dram(self, name, shape, dt, kind):
        return self.nc.dram_tensor(name, list(shape), dt, kind=kind)

    def _alias_deps(self, name):
        s = self.alias_sum.get(name)
        if s is None:
            s = {}
            for on in self.alias[name]:
                for k in self.keys_by_name.get(on, ()):
                    toks = list(self.readers.get(k, ()))
                    t = self.last_w.get(k)
                    if t is not None:
                        toks.append(t)
                    for (sk, v) in toks:
                        if s.get(sk, 0) < v:
                            s[sk] = v
            self.alias_sum[name] = s
        return s.items()

    def _deps(self, eng, reads, writes, is_dma=False, psr=()):
        deps = []
        for r in psr:
            for t in self.readers.get(r, ()):
                if t[0] != eng:
                    deps.append(t)
        for r in reads:
            t = self.last_w.get(r)
            if t is not None:
                deps.append(t)
            n = _name(r)
            if n in self.alias:
                deps.extend(self._alias_deps(n))
        for w in writes:
            t = self.last_w.get(w)
            if t is not None:
                deps.append(t)
            for t in self.readers.get(w, ()):
                deps.append(t)
            n = _name(w)
            if n in self.alias:
                deps.extend(self._alias_deps(n))
        wd = self.waited[eng]
        best = {}
        for (k, v) in deps:
            if wd.get(k, 0) >= v:
                continue
            if best.get(k, 0) < v:
                best[k] = v
        waits = []
        for k, v in best.items():
            wd[k] = v
            waits.append((k, v))
        return waits

    def _commit(self, tok, reads, writes):
        for r in reads:
            self.readers.setdefault(r, []).append(tok)
            self.keys_by_name.setdefault(_name(r), set()).add(r)
        for w in writes:
            self.last_w[w] = tok
            self.readers[w] = []
            self.keys_by_name.setdefault(_name(w), set()).add(w)

    def op(self, eng, fn, reads=(), writes=(), accum=False):
        reads = list(reads)
        writes = list(writes)
        psr = [r for r in reads if isinstance(r, tuple) and r[0] == "ps"]
        if accum:
            waits = self._deps(eng, reads, [], psr=psr)
        else:
            waits = self._deps(eng, reads, writes, psr=psr)
        self.seq[eng] += 1
        tok = (eng, self.seq[eng])
        self.ops[eng].append((waits, fn, (eng, 1)))
        self._commit(tok, reads, writes)
        return tok

    def dma(self, q, out, in_, reads=(), writes=(), **kw):
        reads = list(reads)
        writes = list(writes)
        i = self.dma_slot_next[q]
        self.dma_slot_next[q] = (i + 1) % self.nslots[q]
        key = ("dma", q, i)
        waits = self._deps(q, reads, writes, is_dma=True)
        pv = self.dma_slot_val[key]
        if pv > 0 and self.waited[q].get(key, 0) < pv:
            self.waited[q][key] = pv
            waits.append((key, pv))
        self.dma_slot_val[key] = pv + 16
        tok = (key, pv + 16)

        def fn(e, out=out, in_=in_, kw=kw):
            return e.dma_start(out=out, in_=in_, **kw)

        self.ops[q].append((waits, fn, (key, 16)))
        self._commit(tok, reads, writes)
        return tok

    def wait_all(self, eng, toks):
        waits = []
        for (k, v) in toks:
            if self.waited[eng].get(k, 0) < v:
                self.waited[eng][k] = v
                waits.append((k, v))
        self.ops[eng].append((waits, None, None))

    def bank(self):
        b = self.nbank
        self.nbank = (b + 1) % 8
        return b

    def dump(self, name, t, keys):
        d = self.dram("dbg_" + name, list(t.shape), t.dtype, "ExternalOutput")
        tok = self.dma("sp", d.ap(), t[:], reads=keys)
        self.dumps.append(tok)

    def build(self):
        nc = self.nc
        sems = self.sems
        ops = self.ops
        with nc.Block() as block:
            def emit(e, lst):
                for (waits, fn, inc) in lst:
                    for (k, v) in waits:
                        e.wait_ge(sems[k], v)
                    if fn is None:
                        continue
                    ins = fn(e)
                    ins.then_inc(sems[inc[0]], inc[1])

            @block.tensor
            def _(e):
                emit(e, ops["pe"])

            @block.scalar
            def _(e):
                emit(e, ops["act"])

            @block.vector
            def _(e):
                emit(e, ops["dve"])

            @block.gpsimd
            def _(e):
                emit(e, ops["pool"])

            @block.sync
            def _(e):
                emit(e, ops["sp"])
        self.es.close()
        return nc


def bc(ap_, shape):
    return ap_.to_broadcast(list(shape))


def build(stage=99, debug=False):
    kb = KB()
    nc = kb.nc
    inp = lambda n, s: kb.dram(n, s, F32, "ExternalInput").ap()
    x_d = inp("x", [L, D])
    ctx_d = inp("ctx", [CTXL, D])
    vecs_d = inp("vecs", [128, 4, 8])
    badafm_d = inp("bada_fm", [128, 48])
    badag_d = inp("bada_g", [128, 2, 1024])
    qkcw_d = inp("qkcw", [128, 8, 4])
    ffcw_d = inp("ffcw", [128, NCG, 4])
    gateb_d = inp("gate_b", [128, 16])
    gain_d = inp("gain", [128, 512])
    sink_d = inp("sink", [128, 8])
    fnorm_d = inp("fnorm", [128, D])
    rope_d = inp("rope", [128, NT, 2, 64])
    cst_d = inp("consts", [128, 6, 128])
    wada_d = inp("w_ada", [D, 6 * D])
    win_d = inp("w_in", [D, INC])
    wout_d = inp("w_out", [D, D])
    wup_d = inp("w_up", [D, 2 * DFF])
    wdn_d = inp("w_down", [DFF, D])
    out_d = kb.dram("out", [L, D], F32, "ExternalOutput").ap()

    wada_v = wada_d.rearrange("(k p) c -> p k c", p=128)
    win_v = win_d.rearrange("(k p) c -> p k c", p=128)
    wout_v = wout_d.rearrange("(k p) c -> p k c", p=128)
    wup_v = wup_d.rearrange("(k p) c -> p k c", p=128)
    wdn_v = wdn_d.rearrange("(k p) c -> p k c", p=128)
    x_v = x_d.rearrange("(t p) d -> t p d", p=128)
    ctx_v = ctx_d.rearrange("(t p) d -> t p d", p=128)
    out_v = out_d.rearrange("(t p) d -> t p d", p=128)

    ps = [kb.ps("ps%d" % b, [128, 512], F32) for b in range(8)]
    psb = [p[:].bitcast(BF16) for p in ps]

    def PS(b):
        return ("ps", b)

    def AT(t):
        return [("actT", t, "d"), ("actT", t, "a")]

    def HC(t):
        return [("hxTc", t, "d"), ("hxTc", t, "a")]

    cstf = kb.sb("cstf", [128, 6, 128], F32)
    identb = kb.sb("identb", [128, 128], BF16)
    mnegb = kb.sb("mnegb", [128, 2, 512], BF16)
    vecs = kb.sb("vecs", [128, 4, 8], F32)
    badafm = kb.sb("badafm", [128, 48], F32)
    qkcw = kb.sb("qkcw", [128, 8, 4], F32)
    ffcw = kb.sb("ffcw", [128, NCG, 4], F32)
    modfm = kb.sb("modfm", [128, 6, 8, 2], F32)
    scsh = kb.sb("scsh", [128, 6, 8], F32)
    sc2 = kb.sb("sc2", [128, 8, 2], BF16)
    silrep = kb.sb("silrep", [128, 8, 128], BF16)
    stat = kb.sb("stat", [128, 3, NTT + 2 * NT], F32)
    nhalf = kb.sb("nhalf", [128, 64], F32)
    epsb = kb.sb("epsb", [128, 8], F32)
    GAINH_AT = kb.mark()
    gainh = kb.sb("gainh", [128, 512], F32)
    esink = kb.sb("esink", [128, 8], F32)
    small = kb.sb("small", [128, 64], F32)
    actT = kb.sb("actT", [128, KC, L], BF16)
    P_END = kb.mark()
    tI, tU, tL, tO = 0, 1, 2, 3


    kb.dma("sp", cstf[:], cst_d, writes=["cstf"])
    kb.dma("sp", vecs[:], vecs_d, writes=["vecs"])
    kb.dma("sp", badafm[:], badafm_d, writes=["badafm"])
    kb.dma("sp", qkcw[:], qkcw_d, writes=["qkcw"])
    kb.dma("sp", ffcw[:], ffcw_d, writes=["ffcw"])
    kb.dma("sp", gainh[:], gain_d, writes=["gainh"])
    kb.dma("sp", esink[:], sink_d, writes=["esink"])
    kb.op("dve", lambda e: e.tensor_copy(identb[:], cstf[:, tI, :]), reads=["cstf"], writes=["identb"])
    kb.op("dve", lambda e: e.tensor_copy(mnegb[:, 0, :].rearrange("p (r c) -> p r c", r=4),
                                         cstf[:, tL, :].unsqueeze(1).to_broadcast([128, 4, 128])),
          reads=["cstf"], writes=["mnegb"])
    kb.op("dve", lambda e: e.tensor_copy(mnegb[:, 1, :].rearrange("p (r c) -> p r c", r=4),
                                         cstf[:, tU, :].unsqueeze(1).to_broadcast([128, 4, 128])),
          reads=["cstf"], writes=["mnegb"])
    kb.op("pool", lambda e: e.memset(nhalf[:], -0.5), writes=["nhalf"])
    kb.op("pool", lambda e: e.memset(epsb[:], EPS), writes=["epsb"])
    if debug:
        kb.op("pool", lambda e: e.memset(modfm[:], 0.0), writes=[("modfm", q) for q in range(6)])
    kb.op("act", lambda e: e.activation(esink[:], esink[:], AF.Exp), reads=["esink"], writes=["esink"])

    m_ph = kb.mark()
    wab = [kb.sb("wab%d" % i, [128, KC, 1024], BF16, key="wab") for i in range(2)]
    tmp0 = kb.sb("tmp0", [128, 2, 8], F32)
    tmp1 = kb.sb("tmp1", [128, 2, 8], F32)
    kb.op("act", lambda e: e.activation(tmp0[:], vecs[:, 0:2, :], AF.Tanh, scale=0.5), reads=["vecs"], writes=["tmp0"])
    kb.op("dve", lambda e: e.scalar_tensor_tensor(tmp1[:], tmp0[:], 1.0, vecs[:, 0:2, :], ALU.add, ALU.mult),
          reads=["tmp0", "vecs"], writes=["tmp1"])
    kb.op("dve", lambda e: e.tensor_scalar(sc2[:].rearrange("p k c -> p c k"), tmp1[:], 0.5, None, ALU.mult),
          reads=["tmp1"], writes=["sc2"])
    kb.op("dve", lambda e: e.tensor_copy(silrep[:], sc2[:, :, 0:1].to_broadcast([128, 8, 128])),
          reads=["sc2"], writes=["silrep"])

    def load_wada(q):
        buf = wab[q % 2]
        kb.dma("pool", buf[:], wada_v[:, :, q * 1024:(q + 1) * 1024], writes=[("wab", q % 2)])
        return buf

    def adaln_fm(q, buf, bkey=None):
        bkey = bkey or ("wab", q % 2)
        b = kb.bank()

        def mm(e):
            ins = None
            for j in range(8):
                for k in range(KC):
                    ins = e.matmul(ps[b][:, j * 2:(j + 1) * 2], buf[:, k, j * 128:(j + 1) * 128], sc2[:, k, :],
                                   start=(k == 0), stop=(k == KC - 1))
            return ins
        kb.op("pe", mm, reads=[bkey, "sc2"], writes=[PS(b)])
        kb.op("dve", lambda e: e.tensor_tensor(modfm[:, q, :, :], ps[b][:, 0:16].rearrange("p (j c) -> p j c", c=2),
                                               badafm[:, q * 8:(q + 1) * 8].unsqueeze(2).to_broadcast([128, 8, 2]), ALU.add),
              reads=[PS(b), "badafm"], writes=[("modfm", q)])

    wa_bufs = [load_wada(q) for q in (0, 1)]
    kb.op("pool", lambda e: e.tensor_scalar(gainh[:], gainh[:], 0.5, None, ALU.mult), reads=["gainh"], writes=["gainh"])
    for q in (0, 1):
        adaln_fm(q, wa_bufs[q])
    nm = vecs[:, 2, :]
    nf = vecs[:, 3, :]
    kb.op("dve", lambda e: e.scalar_tensor_tensor(scsh[:, 0, :], modfm[:, 1, :, 0], 1.0, nm, ALU.add, ALU.mult),
          reads=[("modfm", 1), "vecs"], writes=[("scsh", 0)])
    kb.op("dve", lambda e: e.tensor_copy(scsh[:, 1, :], modfm[:, 0, :, 0]), reads=[("modfm", 0)], writes=[("scsh", 1)])
    kb.op("dve", lambda e: e.scalar_tensor_tensor(scsh[:, 2, :], modfm[:, 1, :, 1], 1.0, nm, ALU.add, ALU.mult),
          reads=[("modfm", 1), "vecs"], writes=[("scsh", 2)])
    kb.op("dve", lambda e: e.tensor_copy(scsh[:, 3, :], modfm[:, 0, :, 1]), reads=[("modfm", 0)], writes=[("scsh", 3)])
    kb.reset(m_ph)

    qT = kb.sb("qT", [128, 4, L], BF16)
    kT = kb.sb("kT", [128, 4, L], BF16)
    kTc = kb.sb("kTc", [128, 4, CTXL], BF16)
    NLO = 8
    v1lo = kb.sb("v1lo", [128, NCT + NLO, 4, 129], BF16)
    oglo = kb.sb("oglo", [128, NLO, 512], BF16)
    qrlo = kb.sb("qrlo", [128, NLO, 512], BF16)
    kaT = kb.sb("kaT", [128, NTT * 128], BF16)
    va1 = kb.sb("va1", [128, NTT, 2, 65], BF16)
    gsb = kb.sb("gsb", [128, NTT, 16], F32)
    V1HI_AT = kb.mark()
    v1hi = kb.sb("v1hi", [128, NT - NLO, 4, 129], BF16)
    OGHI_AT = kb.mark()
    oghi = kb.sb("oghi", [128, NT - NLO, 512], BF16)
    QRHI_AT = kb.mark()
    qrhi = kb.sb("qrhi", [128, NT - NLO, 512], BF16)

    def V1(ti):
        return v1lo[:, ti, :, :] if ti < NCT + NLO else v1hi[:, ti - NCT - NLO, :, :]

    def V1K(ti):
        return ("v1lo", ti) if ti < NCT + NLO else ("v1hi", ti)

    def V1O(ti):
        return ("v1lo", "ones") if ti < NCT + NLO else ("v1hi", "ones")

    def OG(t):
        return oglo[:, t, :] if t < NLO else oghi[:, t - NLO, :]

    def OGK(t):
        return ("oglo", t) if t < NLO else ("oghi", t)

    def QR(t):
        return qrlo[:, t, :] if t < NLO else qrhi[:, t - NLO, :]

    def QRK(t):
        return ("qrlo", t) if t < NLO else ("qrhi", t)

    P1_KEEP = kb.mark()
    usb = [kb.sb("usb%d" % i, [128, L + 2], F32, key="usb") for i in range(2)]
    usbc = [kb.sb("usbc%d" % i, [128, CTXL + 2], F32, key="usbc") for i in range(2)]
    cacc = [kb.sb("cacc%d" % i, [128, 512], F32, key="cacc") for i in range(2)]
    cth = [kb.sb("cth%d" % i, [128, 512], F32, key="cth") for i in range(2)]
    krb = [kb.sb("krb%d" % i, [128, 128], BF16, key="krb") for i in range(2)]
    hxTc = kb.sb("hxTc", [128, KC, CTXL], BF16)
    kb.atop = max(kb.atop, P1_KEEP + 32256)
    A_END = kb.mark()
    ropet2 = kb.sb("ropet2", [128, NT, 2, 64], F32)
    gbb = kb.sb("gbb", [128, 16], F32)
    kb.dma("sp", ropet2[:], rope_d, writes=["ropet2"])
    kb.dma("sp", gbb[:], gateb_d, writes=["gbb"])
    P1_END = kb.mark()

    NXB = 3
    xbuf = [kb.sb("xbuf%d" % i, [128, D], F32, key="xbuf") for i in range(NXB)]
    xnb = [kb.sb("xnb%d" % i, [128, D], BF16, key="xnb") for i in range(NXB)]
    sqj = kb.sb("sqj", [128, D], BF16)
    kb.op("pool", lambda e: e.memset(v1lo[:, :, :, 128:129], 1.0), writes=[("v1lo", "ones")])
    kb.op("pool", lambda e: e.memset(v1hi[:, :, :, 128:129], 1.0), writes=[("v1hi", "ones")])
    kb.op("pool", lambda e: e.memset(va1[:, :, :, 64:65], 1.0), writes=[("va1", "ones")])

    def rstd_ops(si):
        if F_ARSTD:
            kb.op("act", lambda e: e.activation(stat[:, 1, si:si + 1], stat[:, 0, si:si + 1], AF.Ln, scale=1.0 / D, bias=epsb[:, 0:1]),
                  reads=[("stat0", si), "epsb"], writes=[("stat1", si)])
            kb.op("act", lambda e: e.activation(stat[:, 2, si:si + 1], stat[:, 1, si:si + 1], AF.Exp, scale=-0.5),
                  reads=[("stat1", si)], writes=[("stat2", si)])
        else:
            kb.op("pool", lambda e: e.tensor_scalar(stat[:, 1, si:si + 1], stat[:, 0, si:si + 1], 1.0 / D, EPS, ALU.mult, ALU.add),
                  reads=[("stat0", si)], writes=[("stat1", si)])
            kb.op("pool", lambda e: e.tensor_tensor(stat[:, 2, si:si + 1], stat[:, 1, si:si + 1], nhalf[:, 0:1], ALU.pow),
                  reads=[("stat1", si), "nhalf"], writes=[("stat2", si)])

    def norm_tile(src_ap, si, xb, sc_i, sh_i, dst, dst_tok0, dst_key):
        i2 = si % NXB
        kb.dma("sp", xbuf[i2][:], src_ap, writes=[("xbuf", i2)])
        kb.op("act", lambda e: e.activation(sqj[:], xbuf[i2][:], AF.Square, accum_out=stat[:, 0, si:si + 1]),
              reads=[("xbuf", i2)], writes=["sqj", ("stat0", si)])
        rstd_ops(si)

        def rest():
            kb.op("pool", lambda e: e.tensor_scalar(xnb[i2][:], xbuf[i2][:], stat[:, 2, si:si + 1], None, ALU.mult),
                  reads=[("xbuf", i2), ("stat2", si)], writes=[("xnb", i2)])
            tr_evac(xnb[i2], ("xnb", i2), sc_i, sh_i, dst, dst_tok0, dst_key, NDV=7)
        return rest

    NDV = 6

    def tr_evac(xn, xkey, sc_i, sh_i, dst, dst_tok0, dst_key, NDV=6):
        ba = kb.bank()
        bb = kb.bank()

        def tr(e):
            ins = None
            for c in range(KC):
                o = psb[ba][:, c * 128:(c + 1) * 128] if c < NDV else psb[bb][:, (c - NDV) * 128:(c - NDV + 1) * 128]
                ins = e.transpose(o, xn[:, c * 128:(c + 1) * 128], identb[:])
            return ins
        kb.op("pe", tr, reads=[xkey, "identb"], writes=[PS(ba), PS(bb)])
        pend_ev.append(lambda: evac_part(ba, bb, sc_i, sh_i, dst, dst_tok0, dst_key, NDV))
        if len(pend_ev) > 1:
            pend_ev.pop(0)()

    pend_ev = []

    def flush_ev():
        while pend_ev:
            pend_ev.pop(0)()

    def evac_part(ba, bb, sc_i, sh_i, dst, dst_tok0, dst_key, NDV):
        for c in range(KC):
            if c < NDV:
                kb.op("dve", lambda e, c=c: e.tensor_scalar(dst[:, c, dst_tok0:dst_tok0 + 128], psb[ba][:, c * 128:(c + 1) * 128],
                                                            scsh[:, sc_i, c:c + 1], scsh[:, sh_i, c:c + 1], ALU.mult, ALU.add),
                      reads=[PS(ba), ("scsh", sc_i), ("scsh", sh_i)], writes=[dst_key + ("d",)])
            else:
                kb.op("act", lambda e, c=c: e.activation(dst[:, c, dst_tok0:dst_tok0 + 128], psb[bb][:, (c - NDV) * 128:(c - NDV + 1) * 128],
                                                         AF.Identity, bias=scsh[:, sh_i, c:c + 1], scale=scsh[:, sc_i, c:c + 1]),
                      reads=[PS(bb), ("scsh", sc_i), ("scsh", sh_i)], writes=[dst_key + ("a",)])

    pend_n = []
    for ci in range(NCT):
        pend_n.append(norm_tile(ctx_v[ci], ci, None, 2, 3, hxTc, ci * 128, ("hxTc", ci)))
    for t in range(NT):
        pend_n.append(norm_tile(x_v[t], NCT + t, None, 0, 1, actT, t * 128, ("actT", t)))
        if len(pend_n) >= NXB:
            pend_n.pop(0)()
    while pend_n:
        pend_n.pop(0)()
    flush_ev()
    if debug:
        kb.dump("hxT", actT, [k_ for t in range(NT) for k_ in AT(t)])
        kb.dump("hxTc", hxTc, [k_ for t in range(NCT) for k_ in HC(t)])
    if stage <= 1:
        return finish(kb)

    kb.reset(P1_END)
    wib = [kb.sb("wib%d" % i, [128, KC, 512], BF16, key="wib") for i in range(2)]
    rt1 = [kb.sb("rt1_%d" % i, [128, 512], F32, key="rt1") for i in range(2)]
    rt2 = [kb.sb("rt2_%d" % i, [128, 512], F32, key="rt2") for i in range(2)]
    for i in range(2):
        kb.op("pool", lambda e, i=i: e.memset(usb[i][:, 0:1], 0.0), writes=[("usb", i, "h0")])
        kb.op("pool", lambda e, i=i: e.memset(usb[i][:, L + 1:L + 2], 0.0), writes=[("usb", i, "h1")])
    for i in range(2):
        kb.op("pool", lambda e, i=i: e.memset(usbc[i][:, 0:1], 0.0), writes=[("usbc", i, "h0")])
        kb.op("pool", lambda e, i=i: e.memset(usbc[i][:, CTXL + 1:CTXL + 2], 0.0), writes=[("usbc", i, "h1")])

    WIN = {0: (0, 512), 1: (512, 512), 2: (2560, 272), 3: (1024, 512), 4: (1536, 512), 5: (2048, 512)}
    win_loaded = set()

    def prefetch_win(g):
        if g in WIN and g not in win_loaded:
            win_loaded.add(g)
            load_win(g, *WIN[g])

    def load_win(g, c0, ncols):
        buf = wib[g % 2]
        kb.dma("pool", buf[:, :, 0:ncols], win_v[:, :, c0:c0 + ncols], writes=[("wib", g % 2)])
        return buf

    cnt = [0]

    ring = [(cacc[0], ("cacc", 0)), (cacc[1], ("cacc", 1)), (cth[0], ("cth", 0)), (cth[1], ("cth", 1))]
    pend_silu = []

    def conv_silu(u, n_tok, j, dst_fn, dst_key_fn):
        def piece(p0):
            n = min(512, n_tok - p0)
            acc, akey = ring[cnt[0] % 4]
            cnt[0] += 1
            ukey = u[1]
            ut = u[0]
            kb.op("pool", lambda e: e.tensor_scalar(acc[:, 0:n], ut[:, 1 + p0:1 + p0 + n], qkcw[:, j, 1:2], qkcw[:, j, 3:4],
                                                    ALU.mult, ALU.add),
                  reads=ukey + ["qkcw"], writes=[akey])
            while len(pend_silu) > 1:
                pend_silu.pop(0)()
            kb.op("dve", lambda e: e.scalar_tensor_tensor(acc[:, 0:n], ut[:, p0:p0 + n], qkcw[:, j, 0:1], acc[:, 0:n], ALU.mult, ALU.add),
                  reads=ukey + ["qkcw", akey], writes=[akey])
            kb.op("dve", lambda e: e.scalar_tensor_tensor(acc[:, 0:n], ut[:, 2 + p0:2 + p0 + n], qkcw[:, j, 2:3], acc[:, 0:n], ALU.mult, ALU.add),
                  reads=ukey + ["qkcw", akey], writes=[akey])
            dst = dst_fn(p0, n)
            dkey = dst_key_fn(p0)
            pend_silu.append(lambda: kb.op("act", lambda e: e.activation(dst, acc[:, 0:n], AF.Silu), reads=[akey], writes=[dkey]))
        for p0 in range(0, n_tok, 512):
            piece(p0)

    pend_conv = []

    def fm_chunk(g, jj, buf):
        j = g * 4 + jj
        u2 = j % 2
        for tg in range(4):
            b = kb.bank()

            def mm(e, b=b, tg=tg):
                ins = None
                for k in range(KC):
                    ins = e.matmul(ps[b][:, :], buf[:, k, jj * 128:(jj + 1) * 128], actT[:, k, tg * 512:(tg + 1) * 512],
                                   start=(k == 0), stop=(k == KC - 1))
                return ins
            kb.op("pe", mm, reads=[("wib", g % 2)] + [k_ for t in range(tg * 4, tg * 4 + 4) for k_ in AT(t)], writes=[PS(b)])
            kb.op("act", lambda e, b=b, tg=tg: e.copy(usb[u2][:, 1 + tg * 512:1 + (tg + 1) * 512], ps[b][:, :]),
                  reads=[PS(b)], writes=[("usb", u2, tg)])
        ukeys = [("usb", u2, x_) for x_ in (0, 1, 2, 3, "h0", "h1")]
        if g == 1:
            b = kb.bank()

            def mmc(e, b=b):
                ins = None
                for k in range(KC):
                    ins = e.matmul(ps[b][:, 0:CTXL], buf[:, k, jj * 128:(jj + 1) * 128], hxTc[:, k, :],
                                   start=(k == 0), stop=(k == KC - 1))
                return ins
            kb.op("pe", mmc, reads=[("wib", g % 2)] + HC(0) + HC(1), writes=[PS(b)])
            kb.op("act", lambda e, b=b: e.copy(usbc[u2][:, 1:1 + CTXL], ps[b][:, 0:CTXL]), reads=[PS(b)], writes=[("usbc", u2)])
        while pend_conv:
            pend_conv.pop(0)()

        def conv():
            if g == 0:
                conv_silu((usb[u2], ukeys), L, j, lambda p0, n: qT[:, jj, p0:p0 + n], lambda p0: ("qT", jj, p0 // 512))
            else:
                conv_silu((usb[u2], ukeys), L, j, lambda p0, n: kT[:, jj, p0:p0 + n], lambda p0: ("kT", jj, p0 // 512))
                conv_silu((usbc[u2], [("usbc", u2), ("usbc", u2, "h0"), ("usbc", u2, "h1")]), CTXL, j,
                          lambda p0, n: kTc[:, jj, p0:p0 + n], lambda p0: ("kTc", jj))
        pend_conv.append(conv)

    for g in range(2):
        prefetch_win(g)
        buf = wib[g % 2]
        if g == 0:
            prefetch_win(1)
        for jj in range(4):
            fm_chunk(g, jj, buf)
            if g == 1 and jj == 1:
                prefetch_win(2)
    while pend_conv:
        pend_conv.pop(0)()
    while pend_silu:
        pend_silu.pop(0)()
    if debug:
        kb.dump("qT", qT, [("qT", jj, p) for jj in range(4) for p in range(4)])
        kb.dump("kT", kT, [("kT", jj, p) for jj in range(4) for p in range(4)])
        kb.dump("kTc", kTc, [("kTc", jj) for jj in range(4)])

    def hx_tile(ti):
        if ti < NCT:
            return (lambda k: hxTc[:, k, ti * 128:(ti + 1) * 128]), HC(ti)
        t = ti - NCT
        return (lambda k: actT[:, k, t * 128:(t + 1) * 128]), AT(t)

    def tm_group(g, c0, ncols, tiles, evac, hook=None):
        assert WIN[g] == (c0, ncols)
        prefetch_win(g)
        buf = wib[g % 2]
        for n_, ti in enumerate(tiles):
            if n_ == 2:
                prefetch_win(g + 1)
            lhs, hkey = hx_tile(ti)
            b = kb.bank()

            def mm(e, b=b, lhs=lhs, buf=buf):
                ins = None
                for k in range(KC):
                    ins = e.matmul(ps[b][:, 0:ncols], lhs(k), buf[:, k, 0:ncols], start=(k == 0), stop=(k == KC - 1))
                return ins
            kb.op("pe", mm, reads=[("wib", g % 2)] + hkey, writes=[PS(b)])
            evac(ti, b)
            if hook is not None:
                hook()

    def ev_v(ti, b):
        kb.op("act", lambda e: e.copy(V1(ti)[:, :, 0:128], ps[b][:, :].rearrange("p (h d) -> p h d", h=4)),
              reads=[PS(b)], writes=[V1K(ti)])

    def ev_o(ti, b):
        t = ti - NCT
        i2 = t % 2
        kb.op("act", lambda e: e.activation(rt1[i2][:], ps[b][:, :], AF.Tanh, scale=0.5), reads=[PS(b)], writes=[("rt1", i2)])
        kb.op("dve", lambda e: e.scalar_tensor_tensor(OG(t), rt1[i2][:], 1.0, gainh[:], ALU.add, ALU.mult),
              reads=[("rt1", i2), "gainh"], writes=[OGK(t)])

    def rope(src, nh, t, dst, skey, dkey, i2):
        cosb = ropet2[:, t, 0, :].unsqueeze(1).to_broadcast([128, nh, 64])
        s4 = src.rearrange("p (h a f q) -> p h a f q", h=nh, a=2, f=2)
        t1, t2 = rt1[i2], rt2[i2]
        t14 = t1[:, 0:nh * 64].rearrange("p (h a f q) -> p h a f q", h=nh, a=2, f=2)
        t24 = t2[:, 0:nh * 64].rearrange("p (h a f q) -> p h a f q", h=nh, a=2, f=2)
        d4 = dst.rearrange("p (h a f q) -> p h a f q", h=nh, a=2, f=2)
        sin4 = ropet2[:, t, 1, :].rearrange("p (a f q) -> p a f q", a=2, f=2)
        kb.op("dve", lambda e: e.tensor_tensor(t1[:, 0:nh * 64].rearrange("p (h d) -> p h d", h=nh),
                                               src.rearrange("p (h d) -> p h d", h=nh), cosb, ALU.mult),
              reads=[skey, "ropet2"], writes=[("rt1", i2)])
        for f in range(2):
            sb_ = sin4[:, :, f, :].unsqueeze(1).to_broadcast([128, nh, 2, 16])
            kb.op("dve", lambda e, f=f, sb_=sb_: e.tensor_tensor(t24[:, :, :, f, :], s4[:, :, :, 1 - f, :], sb_, ALU.mult),
                  reads=[skey, "ropet2"], writes=[("rt2", i2, f)])
        kb.op("pool", lambda e: e.tensor_tensor(dst, t1[:, 0:nh * 64], t2[:, 0:nh * 64], ALU.add),
              reads=[("rt1", i2), ("rt2", i2, 0), ("rt2", i2, 1)], writes=[dkey])

    def ev_q(ti, b):
        t = ti - NCT
        rope(ps[b][:, :], 8, t, QR(t), PS(b), QRK(t), t % 2)

    def ev_kvg(ti, b):
        i2 = ti % 2
        if ti < NCT:
            kb.op("act", lambda e: e.copy(krb[i2][:], ps[b][:, 0:128]), reads=[PS(b)], writes=[("krb", i2)])
        else:
            rope(ps[b][:, 0:128], 2, ti - NCT, krb[i2][:], PS(b), ("krb", i2), i2)
        kb.op("act", lambda e: e.copy(va1[:, ti, :, 0:64], ps[b][:, 128:256].rearrange("p (h d) -> p h d", h=2)),
              reads=[PS(b)], writes=[("va1", ti)])
        kb.op("dve", lambda e: e.tensor_tensor(gsb[:, ti, :], ps[b][:, 256:272], gbb[:], ALU.add),
              reads=[PS(b), "gbb"], writes=[("gsb", ti)])
        def do_tr():
            b2 = kb.bank()
            kb.op("pe", lambda e: e.transpose(psb[b2][:, 0:128], krb[i2][:], identb[:]), reads=[("krb", i2), "identb"], writes=[PS(b2)])
            kb.op("dve", lambda e: e.tensor_copy(kaT[:, ti * 128:(ti + 1) * 128], psb[b2][:, 0:128]),
                  reads=[PS(b2)], writes=[("kaT", ti)])
        if pend_tr:
            pend_tr.pop(0)()
        pend_tr.append(do_tr)

    pend_tr = []
    tm_group(2, 2560, 272, range(NTT), ev_kvg)
    while pend_tr:
        pend_tr.pop(0)()
    tm_group(3, 1024, 512, range(NTT), ev_v)
    kb.reset(P1_KEEP)
    lfs = kb.sb("lfs", [128, NTT, 2, 4], F32)
    gd = kb.sb("gd", [128, 3, NTT, 2, 4], F32)
    tmpc = kb.sb("tmpc", [128, NTT, 2, 4], F32)
    Cst = kb.sb("Cst", [128, 2, 4, 129], F32)
    Cbf = kb.sb("Cbf", [128, 2, 4, 129], BF16)
    C0lo = kb.sb("C0lo", [128, NLO, 4, 129], BF16)
    C0HI_AT = kb.mark()
    C0hi = kb.sb("C0hi", [128, NT - NLO, 4, 129], BF16)

    def C0(t):
        return C0lo[:, t, :, :] if t < NLO else C0hi[:, t - NLO, :, :]

    def C0K(t, p_):
        return ("C0lo", t, p_) if t < NLO else ("C0hi", t, p_)
    ktm = [kb.sb("ktm%d" % i, [128, 4, 128], BF16, key="ktm") for i in range(2)]
    vwb = [[kb.sb("vwb%d_%d" % (d_, i), [128, 4, 129], BF16, key="vwb") for i in range(2)] for d_ in range(2)]
    gview = gsb[:, :, :].rearrange("p t (g h) -> p t g h", g=4)
    GS = [("gsb", ti) for ti in range(NTT)]
    for d_ in range(2):
        kb.op("act", lambda e, d_=d_: e.activation(lfs[:, :, d_, :], gview[:, :, 1 + 2 * d_, :], AF.Exp, scale=-1.0),
              reads=GS, writes=[("lfs", d_)])
    kb.op("act", lambda e: e.activation(lfs[:], lfs[:], AF.Ln, bias=1.0), reads=[("lfs", 0), ("lfs", 1)], writes=[("lfs", 0), ("lfs", 1)])
    bg = kb.bank()

    def mmg(e):
        e.matmul(ps[bg][:, 0:72], cstf[:, tU, :], lfs[:, :, 0, :], start=True, stop=True)
        e.matmul(ps[bg][:, 72:144], cstf[:, tL, :], lfs[:, :, 1, :], start=True, stop=True)
        return e.matmul(ps[bg][:, 144:288], cstf[:, tO, :], lfs[:], start=True, stop=True)
    kb.op("pe", mmg, reads=["cstf", ("lfs", 0), ("lfs", 1)], writes=[PS(bg)])
    for d_ in range(2):
        cum = ps[bg][:, 72 * d_:72 * (d_ + 1)].rearrange("p (t h) -> p t h", h=4)
        kb.op("act", lambda e, d_=d_, cum=cum: e.activation(gd[:, 0, :, d_, :], cum, AF.Exp, bias=-LNK),
              reads=[PS(bg)], writes=[("gd", 0, d_)])
        kb.op("dve", lambda e, d_=d_, cum=cum: e.tensor_tensor(tmpc[:, :, d_, :], cum, gview[:, :, 2 * d_, :], ALU.add),
              reads=[PS(bg)] + GS, writes=[("tmpc", d_)])
    kb.op("act", lambda e: e.activation(gd[:, 1, :, :, :], tmpc[:], AF.Exp), reads=[("tmpc", 0), ("tmpc", 1)], writes=[("gd", 1)])
    kb.op("act", lambda e: e.activation(gd[:, 2, :, :, :], ps[bg][:, 144:288].rearrange("p (t d h) -> p t d h", d=2, h=4), AF.Exp, scale=-1.0),
          reads=[PS(bg)], writes=[("gd", 2)])
    GD = [("gd", 0, 0), ("gd", 0, 1), ("gd", 1), ("gd", 2)]
    kb.op("pool", lambda e: e.memset(Cst[:], 0.0), writes=[("Cst", d_, p_) for d_ in range(2) for p_ in range(2)])
    kb.op("pool", lambda e: e.memset(Cbf[:], 0.0), writes=[("Cbf", d_, p_) for d_ in range(2) for p_ in range(2)])

    def ktile(ti):
        if ti < NCT:
            return (lambda h: kTc[:, h, ti * 128:(ti + 1) * 128]), [("kTc", h) for h in range(4)]
        t = ti - NCT
        return (lambda h: kT[:, h, t * 128:(t + 1) * 128]), [("kT", h, t // 4) for h in range(4)]

    cntk = [0]

    def make_ktm(ti):
        i2 = cntk[0] % 2
        cntk[0] += 1
        kf, kkeys = ktile(ti)
        b = kb.bank()

        def tr(e):
            ins = None
            for h in range(4):
                ins = e.transpose(psb[b][:, h * 128:(h + 1) * 128], kf(h), identb[:])
            return ins
        kb.op("pe", tr, reads=kkeys + ["identb"], writes=[PS(b)])
        kb.op("act", lambda e: e.copy(ktm[i2][:], psb[b][:, 0:512].rearrange("p (h d) -> p h d", h=4)),
              reads=[PS(b)], writes=[("ktm", i2)])
        return i2

    cntv = [0, 0]

    def make_vw(ti, d_, eng="pool"):
        i2 = cntv[d_] % 2
        cntv[d_] += 1
        if eng == "act":
            for h in range(4):
                kb.op("act", lambda e, h=h: e.activation(vwb[d_][i2][:, h, :], V1(ti)[:, h, :], AF.Copy, scale=gd[:, 1, ti, d_, h:h + 1]),
                      reads=[V1K(ti), V1O(ti), ("gd", 1)], writes=[("vwb", d_, i2, h)])
        else:
            kb.op("pool", lambda e: e.tensor_tensor(vwb[d_][i2][:], V1(ti),
                                                    gd[:, 1, ti, d_, :].unsqueeze(2).to_broadcast([128, 4, 129]), ALU.mult),
                  reads=[V1K(ti), V1O(ti), ("gd", 1)], writes=[("vwb", d_, i2, h) for h in range(4)])
        return i2

    def state_update(ti, d_, ik, iv, c0_dst=None, cp="pool"):
        for p_ in range(2):
            b = kb.bank()

            def mm(e, b=b, p_=p_):
                ins = None
                for hh in range(2):
                    h = 2 * p_ + hh
                    ins = e.matmul(ps[b][:, hh * 129:(hh + 1) * 129], ktm[ik][:, h, :], vwb[d_][iv][:, h, :], start=True, stop=True)
                return ins
            kb.op("pe", mm, reads=[("ktm", ik)] + [("vwb", d_, iv, h) for h in range(4)], writes=[PS(b)])
            cs = Cst[:, d_, 2 * p_:2 * p_ + 2, :]
            kb.op("dve", lambda e, b=b, cs=cs: e.tensor_tensor(cs, ps[b][:, 0:258].rearrange("p (h e) -> p h e", e=129), cs, ALU.add),
                  reads=[PS(b), ("Cst", d_, p_)], writes=[("Cst", d_, p_)])
            ebb = gd[:, 2, ti, d_, 2 * p_:2 * p_ + 2].unsqueeze(2).to_broadcast([128, 2, 129])
            kb.op("dve", lambda e, cs=cs, ebb=ebb: e.tensor_tensor(cs, cs, ebb, ALU.mult),
                  reads=[("Cst", d_, p_), ("gd", 2)], writes=[("Cst", d_, p_)])
            if cp == "act":
                cpf = lambda e, o_, i_: e.copy(o_, i_)
            else:
                cpf = lambda e, o_, i_: e.tensor_copy(o_, i_)
            if c0_dst is None or d_ == 1:
                kb.op(cp, lambda e, cs=cs, p_=p_: cpf(e, Cbf[:, d_, 2 * p_:2 * p_ + 2, :], cs),
                      reads=[("Cst", d_, p_)], writes=[("Cbf", d_, p_)])
            if c0_dst is not None:
                kb.op(cp, lambda e, cs=cs, p_=p_: cpf(e, C0(c0_dst)[:, 2 * p_:2 * p_ + 2, :], cs),
                      reads=[("Cst", d_, p_)], writes=[C0K(c0_dst, p_)])

    chain = []

    def step_a(ti, d_):
        return make_ktm(ti), make_vw(ti, d_, eng="act")

    def step_b(ti, d_, ik, iv):
        if d_ == 0:
            nxt = ti - NCT + 1
            state_update(ti, 0, ik, iv, c0_dst=nxt if nxt >= 0 else None, cp="act")
        else:
            state_update(ti, 1, ik, iv, cp="act")

    seq = [(ti, 0) for ti in range(0, NTT - 1)] + [(1, 1), (0, 1)]
    held = []

    def mk(i):
        def f():
            if held:
                step_b(*held.pop(0))
            if i < len(seq):
                ti, d_ = seq[i]
                ik, iv = step_a(ti, d_)
                held.append((ti, d_, ik, iv))
        return f
    for i in range(len(seq) + 1):
        chain.append(mk(i))

    def chain_hook():
        if chain:
            chain.pop(0)()

    tm_group(4, 1536, 512, range(NCT, NTT), ev_o, hook=chain_hook)
    tm_group(5, 2048, 512, range(NCT, NTT), ev_q, hook=chain_hook)
    while chain:
        chain_hook()
    if debug:
        kb.dump("gd", gd, GD)
        kb.dump("Cbf", Cbf, [("Cbf", d_, p_) for d_ in range(2) for p_ in range(2)])
    if debug:
        kb.dump("kaT", kaT, [("kaT", ti) for ti in range(NTT)])
        kb.dump("va1", va1, [("va1", ti) for ti in range(NTT)] + [("va1", "ones")])
        kb.dump("gsb", gsb, [("gsb", ti) for ti in range(NTT)])
    if stage <= 2:
        return finish(kb)

    kb.reset(A_END)
    sTm = [[kb.sb("sTm%d_%d" % (d_, i), [128, 4, 128], BF16, key="sTm") for i in range(2)] for d_ in range(2)]
    hsum = [kb.sb("hsum%d" % i, [128, 4, 128], F32, key="hsum") for i in range(2)]
    hsqs = [kb.sb("hsq%d" % i, [128, 4, 128], F32, key="hsq") for i in range(2)]
    ytile = [kb.sb("ytile%d" % i, [128, D], BF16, key="ytile") for i in range(2)]
    qaTb = [kb.sb("qaTb0", [128, 4, 128], BF16, key="qaTb", at=GAINH_AT + 1024),
            kb.sb("qaTb1", [128, 4, 128], BF16, key="qaTb")]
    pTbs = [[kb.sb("pTb%d_%d" % (j_, i), [128, 5, 512], BF16, key="pTb") for i in range(2)] for j_ in range(2)]
    fsm = kb.sb("fsm", [128, 2, 16, 8], F32, at=GAINH_AT)
    if debug:
        kb.op("pool", lambda e: e.memset(fsm[:], 0.0), writes=[("fsm", a_, b_) for a_ in range(2) for b_ in range(16)])
    def attention(T, yt, ykey):
        ti = T + NCT
        i2 = T % 2
        pTb = pTbs[i2]
        b = kb.bank()

        def tr(e):
            ins = None
            for g in range(4):
                ins = e.transpose(psb[b][:, g * 128:(g + 1) * 128], QR(T)[:, g * 128:(g + 1) * 128], identb[:])
            return ins
        kb.op("pe", tr, reads=[QRK(T), "identb"], writes=[PS(b)])
        kb.op("act", lambda e: e.copy(qaTb[i2][:], psb[b][:, 0:512].rearrange("p (g t) -> p g t", g=4)),
              reads=[PS(b)], writes=[("qaTb", i2)])
        yield
        kbs = []
        if T > 0:
            kbs.append((ti - 1, 0))
        if T < NT - 1:
            kbs.append((ti + 1, 1))
        kbs += [(0, None), (1, None), (ti, None)]

        def qk(kbi, kblk, mi, hk):
            b = kb.bank()
            kb.op("pe", lambda e: e.matmul(ps[b][:, :], kaT[64 * hk:64 * hk + 64, kblk * 128:(kblk + 1) * 128],
                                           qaTb[i2][64 * hk:64 * hk + 64, :, :], start=True, stop=True),
                  reads=[("kaT", kblk), ("qaTb", i2)], writes=[PS(b)])
            kb.op("act", lambda e: e.activation(pTb[hk][:, kbi, :], ps[b][:, :], AF.Exp, scale=0.125),
                  reads=[PS(b)], writes=[("pTb", i2, hk, kbi)])
            if mi is not None:
                kb.op("pool", lambda e: e.tensor_tensor(pTb[hk][:, kbi, :], pTb[hk][:, kbi, :], mnegb[:, mi, :], ALU.mult),
                      reads=[("pTb", i2, hk, kbi), "mnegb"], writes=[("pTb", i2, hk, kbi)])
        for kbi, (kblk, mi) in enumerate(kbs):
            for hk in range(2):
                qk(kbi, kblk, mi, hk)
            yield

        def head(hk):
            pb = pTb[hk]
            bo = kb.bank()

            def pv(e):
                ins = None
                for g in range(4):
                    for kbi, (kblk, mi) in enumerate(kbs):
                        ins = e.matmul(ps[bo][:, g * 65:(g + 1) * 65], pb[:, kbi, g * 128:(g + 1) * 128], va1[:, kblk, hk, :],
                                       start=(kbi == 0), stop=(kbi == len(kbs) - 1))
                return ins
            kb.op("pe", pv, reads=[("pTb", i2, hk, kbi) for kbi in range(len(kbs))] + [("va1", kblk) for kblk, _ in kbs] + [("va1", "ones")],
                  writes=[PS(bo)])
            pv3 = ps[bo][:, 0:260].rearrange("p (g e) -> p g e", e=65)
            dn = fsm[:, i2, 8 + hk, 0:4]
            kb.op("dve", lambda e: e.tensor_tensor(dn, pv3[:, :, 64], esink[:, 4 * hk:4 * hk + 4], ALU.add),
                  reads=[PS(bo), "esink"], writes=[("fsm", i2, 8 + hk)])
            kb.op("dve", lambda e: e.reciprocal(dn, dn), reads=[("fsm", i2, 8 + hk)], writes=[("fsm", i2, 8 + hk)])
            kb.op("dve", lambda e: e.tensor_tensor(
                yt[:, 512 + 256 * hk:512 + 256 * (hk + 1)].rearrange("p (g d) -> p g d", d=64), pv3[:, :, 0:64],
                dn.unsqueeze(2).to_broadcast([128, 4, 64]), ALU.mult),
                reads=[PS(bo), ("fsm", i2, 8 + hk)], writes=[ykey + ("a", hk)])
        for hk in range(2):
            head(hk)
            yield

    def mlstm_tile(T, yt, ykey):
        ti = T + NCT
        i2 = T % 2
        ik = make_ktm(ti)
        ivf = make_vw(ti, 0)
        ivb = make_vw(ti, 1)
        yield
        bs = kb.bank()

        def mms(e):
            ins = None
            for h in range(4):
                ins = e.matmul(ps[bs][:, h * 128:(h + 1) * 128], kT[:, h, T * 128:(T + 1) * 128], qT[:, h, T * 128:(T + 1) * 128],
                               start=True, stop=True)
            return ins
        qk_keys = [("kT", h, T // 4) for h in range(4)] + [("qT", h, T // 4) for h in range(4)]
        kb.op("pe", mms, reads=qk_keys, writes=[PS(bs)])
        s3 = ps[bs][:, :].rearrange("p (h t) -> p h t", h=4)
        for d_ in range(2):
            tri = cstf[:, tU if d_ == 0 else tL, :].unsqueeze(1).to_broadcast([128, 4, 128])
            kb.op("dve", lambda e, d_=d_, tri=tri: e.tensor_tensor(sTm[d_][i2][:], s3, tri, ALU.mult),
                  reads=[PS(bs), "cstf"], writes=[("sTm", d_, i2)])
        yield
        hs = hsum[i2]
        while T < NT - 1 and not su_done.get(T + 1, False):
            yield
        bd = kb.bank()

        def mmd(e):
            ins = None
            for d_ in range(2):
                iv = ivf if d_ == 0 else ivb
                for h in range(4):
                    g8 = d_ * 4 + h
                    o = ps[bd][:, 2 * g8:2 * g8 + 2]
                    e.matmul(o, sTm[d_][i2][:, h, :], vwb[d_][iv][:, h, 127:129], start=True, stop=False)
                    c0 = C0(T)[:, h, 127:129] if d_ == 0 else Cbf[:, 1, h, 127:129]
                    ins = e.matmul(o, qT[:, h, T * 128:(T + 1) * 128], c0, start=False, stop=True)
            return ins
        ckeys = [C0K(T, 0), C0K(T, 1), ("Cbf", 1, 0), ("Cbf", 1, 1)]
        vkeys = [("vwb", 0, ivf, h) for h in range(4)] + [("vwb", 1, ivb, h) for h in range(4)]
        kb.op("pe", mmd, reads=[("sTm", 0, i2), ("sTm", 1, i2)] + ckeys + vkeys + qk_keys, writes=[PS(bd)])
        fa = fsm[:, i2, 0, 0:8]
        fb = fsm[:, i2, 1, 0:8]
        den8 = ps[bd][:, 0:16].rearrange("p (g c) -> p g c", c=2)[:, :, 1]
        kb.op("dve", lambda e: e.tensor_tensor(fa, den8, gd[:, 0, ti, :, :].rearrange("p d h -> p (d h)"), ALU.max),
              reads=[PS(bd), ("gd", 0, 0), ("gd", 0, 1)], writes=[("fsm", i2, 0)])
        kb.op("dve", lambda e: e.scalar_tensor_tensor(fb, den8, -1.0, fa, ALU.mult, ALU.max),
              reads=[PS(bd), ("fsm", i2, 0)], writes=[("fsm", i2, 1)])
        kb.op("dve", lambda e: e.reciprocal(fb, fb), reads=[("fsm", i2, 1)], writes=[("fsm", i2, 1)])
        yield
        for idx_, d_ in enumerate((1, 0)):
            iv = ivf if d_ == 0 else ivb
            b = kb.bank()

            def mmp(e, b=b, d_=d_, iv=iv):
                ins = None
                for h in range(4):
                    o = ps[b][:, h * 128:(h + 1) * 128]
                    e.matmul(o, sTm[d_][i2][:, h, :], vwb[d_][iv][:, h, 0:128], start=True, stop=False)
                    c0 = C0(T)[:, h, 0:128] if d_ == 0 else Cbf[:, 1, h, 0:128]
                    ins = e.matmul(o, qT[:, h, T * 128:(T + 1) * 128], c0, start=False, stop=True)
                return ins
            ck = [C0K(T, 0), C0K(T, 1)] if d_ == 0 else [("Cbf", 1, 0), ("Cbf", 1, 1)]
            kb.op("pe", mmp, reads=[("sTm", d_, i2)] + ck + [("vwb", d_, iv, h) for h in range(4)] + qk_keys, writes=[PS(b)])
            p3 = ps[b][:, :].rearrange("p (h e) -> p h e", e=128)
            fbc = fb[:, 4 * d_:4 * d_ + 4].unsqueeze(2).to_broadcast([128, 4, 128])
            if idx_ == 0:
                kb.op("dve", lambda e, p3=p3, fbc=fbc: e.tensor_tensor(hs[:], p3, fbc, ALU.mult),
                      reads=[PS(b), ("fsm", i2, 1)], writes=[("hsum", i2, h) for h in range(4)])
                if T > 0:
                    state_update(ti, 1, ik, ivb, cp="act")
                su_done[T] = True
            else:
                for h in range(4):
                    kb.op("dve", lambda e, p3=p3, h=h, d_=d_: e.scalar_tensor_tensor(hs[:, h, :], p3[:, h, :], fb[:, 4 * d_ + h:4 * d_ + h + 1], hs[:, h, :],
                                                                                 ALU.mult, ALU.add),
                          reads=[PS(b), ("fsm", i2, 1), ("hsum", i2, h)], writes=[("hsum", i2, h)])
            yield
        HK = [("hsum", i2, h) for h in range(4)]
        hsq = hsqs[i2]
        kb.op("act", lambda e: e.activation(hsq[:], hs[:], AF.Square), reads=HK, writes=[("hsq", i2)])
        ssq = fsm[:, i2, 10, 0:4]
        kb.op("dve", lambda e: e.tensor_reduce(ssq, hsq[:], AX.X, ALU.add), reads=[("hsq", i2)], writes=[("fsm", i2, 10)])
        kb.op("act", lambda e: e.activation(ssq, ssq, AF.Ln, scale=1.0 / 128, bias=epsb[:, 0:1]), reads=[("fsm", i2, 10), "epsb"], writes=[("fsm", i2, 10)])
        kb.op("act", lambda e: e.activation(ssq, ssq, AF.Exp, scale=-0.5), reads=[("fsm", i2, 10)], writes=[("fsm", i2, 10)])
        for h in range(4):
            kb.op("dve", lambda e, h=h: e.scalar_tensor_tensor(yt[:, h * 128:(h + 1) * 128], hs[:, h, :], ssq[:, h:h + 1],
                                                               OG(T)[:, h * 128:(h + 1) * 128], ALU.mult, ALU.mult),
                  reads=HK + [("fsm", i2, 10), OGK(T)], writes=[ykey + ("m", h)])
        yield

    su_done = {}

    def finish_tile(T):
        yt = ytile[T % 2]
        ykey = ("ytile", T % 2)
        b = kb.bank()

        def try_(e):
            ins = None
            for c in range(KC):
                ins = e.transpose(psb[b][:, c * 128:(c + 1) * 128], yt[:, c * 128:(c + 1) * 128], identb[:])
            return ins
        kb.op("pe", try_, reads=[ykey + ("m", h_) for h_ in range(4)] + [ykey + ("a", 0), ykey + ("a", 1), "identb"], writes=[PS(b)])
        kb.op("act", lambda e: e.copy(actT[:, :, T * 128:(T + 1) * 128], psb[b][:, :].rearrange("p (c t) -> p c t", c=KC)),
              reads=[PS(b)], writes=AT(T))

    woutb = [kb.sb("woutb0", [128, KC, 512], BF16, key="woutb0", at=OGHI_AT),
             kb.sb("woutb1", [128, KC, 512], BF16, key="woutb1", at=QRHI_AT)]
    wabgH = kb.sb("wabgH", [128, KC, 512], BF16, at=V1HI_AT)
    wabgH2 = kb.sb("wabgH2", [128, KC, 512], BF16, at=C0HI_AT)

    def prefetch_p3():
        kb.dma("pool", wabgH[:], wada_v[:, :, 2048:2560], writes=["wabgH"])
        kb.dma("pool", wabgH2[:], wada_v[:, :, 2560:3072], writes=["wabgH2"])
        for half in range(2):
            kb.dma("pool", woutb[half][:], wout_v[:, :, half * 512:(half + 1) * 512], writes=[("woutb%d" % half, 0)])

    pending = list(range(NT - 1, -1, -1))
    active = []
    since = 99
    NFLY = 2 if F_INTER else 1
    while pending or active:
        if pending and len(active) < NFLY and since >= 5:
            T = pending.pop(0)
            active.append([T, [attention(T, ytile[T % 2], ("ytile", T % 2)), mlstm_tile(T, ytile[T % 2], ("ytile", T % 2))]])
            since = 0
        since += 1
        for ent in list(active):
            T, gens = ent
            for g_ in list(gens):
                try:
                    next(g_)
                except StopIteration:
                    gens.remove(g_)
            if not gens:
                finish_tile(T)
                active.remove(ent)
                if T == NLO:
                    prefetch_p3()
    if debug:
        kb.dump("yT", actT, [k_ for t in range(NT) for k_ in AT(t)])
    if stage <= 3:
        return finish(kb)

    kb.reset(P_END)
    x1 = kb.sb("x1", [128, NT, D], F32)
    X1_END = kb.mark()
    assert X1_END <= V1HI_AT
    kb.reset(P1_KEEP)
    gabc = kb.sb("gabc", [128, D], F32)
    gfbc = kb.sb("gfbc", [128, D], F32)
    fnb = kb.sb("fnb", [128, D], F32)
    P3_KEEP = kb.mark()
    wabg = [kb.sb("wabg%d" % i, [128, KC, 1024], BF16, key="wabg") for i in range(2)]
    bgt = kb.sb("bgt", [128, 2, 1024], F32)
    xb3 = [kb.sb("xb3_%d" % i, [128, D], F32, key="xb3") for i in range(2)]
    NX3 = 2
    xn3 = [kb.sb("xn3_%d" % i, [128, D], BF16, key="xn3") for i in range(NX3)]
    sqj3 = kb.sb("sqj3", [128, D], BF16, at=X1_END)
    tmpo = [kb.sb("tmpo%d" % i, [128, 512], F32, key="tmpo") for i in range(2)]
    kb.dma("sp", bgt[:], badag_d, writes=["bgt"])

    def g_half(buf, bkey, gi, half, dst, dkey, c0=None):
        b = kb.bank()
        c0 = half * 512 if c0 is None else c0

        def mm(e):
            ins = None
            for k in range(KC):
                ins = e.matmul(ps[b][:, :], silrep[:, k, :], buf[:, k, c0:c0 + 512], start=(k == 0), stop=(k == KC - 1))
            return ins
        kb.op("pe", mm, reads=["silrep", bkey], writes=[PS(b)])
        kb.op("dve", lambda e: e.tensor_tensor(dst[:, half * 512:(half + 1) * 512], ps[b][:, :],
                                               bgt[:, gi, half * 512:(half + 1) * 512], ALU.add),
              reads=[PS(b), "bgt"], writes=[(dkey, half)])

    def g_piece(gi, bi, dst, dkey):
        for half in range(2):
            g_half(wabg[bi], ("wabg", bi), gi, half, dst, dkey)

    g_half(wabgH, "wabgH", 0, 0, gabc, "gabc", c0=0)
    g_half(wabgH2, "wabgH2", 0, 1, gabc, "gabc", c0=0)
    kb.dma("pool", wabg[1][:], wada_v[:, :, 3072:4096], writes=[("wabg", 1)])
    kb.dma("pool", wabg[0][:], wada_v[:, :, 4096:5120], writes=[("wabg", 0)])
    kb.dma("sp", fnb[:], fnorm_d, writes=["fnb"])
    if debug:
        kb.dump("gabc", gabc, [("gabc", 0), ("gabc", 1)])

    def norm_stats(T, si):
        src = x1[:, T, :]
        kb.op("act", lambda e: e.activation(sqj3[:], src, AF.Square, accum_out=stat[:, 0, si:si + 1]),
              reads=[("x1", T, 0), ("x1", T, 1)], writes=["sqj3", ("stat0", si)])
        rstd_ops(si)

    def norm_sb(T, si):
        i2 = T % NX3
        src = x1[:, T, :]
        kb.op("dve", lambda e: e.tensor_scalar(xn3[i2][:], src, stat[:, 2, si:si + 1], None, ALU.mult),
              reads=[("x1", T, 0), ("x1", T, 1), ("stat2", si)], writes=[("xn3", i2)])
        tr_evac(xn3[i2], ("xn3", i2), 4, 5, actT, T * 128, ("actT", T), NDV=4)

    def outproj(T):
        i2 = T % 2
        kb.dma("sp", xb3[i2][:], x_v[T], writes=[("xb3", i2)])
        for half in range(2):
            b = kb.bank()
            j4 = half

            def mm(e, b=b, half=half):
                ins = None
                for k in range(KC):
                    ins = e.matmul(ps[b][:, :], actT[:, k, T * 128:(T + 1) * 128], woutb[half][:, k, :],
                                   start=(k == 0), stop=(k == KC - 1))
                return ins
            kb.op("pe", mm, reads=AT(T) + [("woutb%d" % half, 0)], writes=[PS(b)])
            kb.op("dve", lambda e, b=b, half=half, j4=j4: e.tensor_tensor(tmpo[j4][:], ps[b][:, :], gabc[:, half * 512:(half + 1) * 512], ALU.mult),
                  reads=[PS(b), ("gabc", half)], writes=[("tmpo", j4)])
            kb.op("pool", lambda e, half=half, j4=j4: e.tensor_tensor(x1[:, T, half * 512:(half + 1) * 512], tmpo[j4][:],
                                                                      xb3[i2][:, half * 512:(half + 1) * 512], ALU.add),
                  reads=[("tmpo", j4), ("xb3", i2)], writes=[("x1", T, half)])

    def extras_a():
        adaln_fm(3, wabg[1], ("wabg", 1))

    def extras_b():
        adaln_fm(4, wabg[0], ("wabg", 0))
        kb.dma("pool", wabg[1][:], wada_v[:, :, 5120:6144], writes=[("wabg", 1)])
        kb.op("dve", lambda e: e.scalar_tensor_tensor(scsh[:, 4, :], modfm[:, 4, :, 0], 1.0, nf, ALU.add, ALU.mult),
              reads=[("modfm", 4), "vecs"], writes=[("scsh", 4)])
        kb.op("dve", lambda e: e.tensor_copy(scsh[:, 5, :], modfm[:, 3, :, 0]), reads=[("modfm", 3)], writes=[("scsh", 5)])

    def extras_c():
        g_piece(1, 1, gfbc, "gfbc")

    LAG = 6
    for T in range(NT):
        outproj(T)
        norm_stats(T, NTT + T)
        if T == 3:
            extras_a()
        if T == 6:
            extras_b()
        if T >= LAG:
            norm_sb(T - LAG, NTT + T - LAG)
    extras_c()
    for T in range(NT - LAG, NT):
        norm_sb(T, NTT + T)
    flush_ev()
    if debug:
        kb.dump("x1", x1, [("x1", T, h_) for T in range(NT) for h_ in range(2)])
        kb.dump("h2T", actT, [k_ for t in range(NT) for k_ in AT(t)])
    if stage <= 4:
        return finish(kb)

    GROUPS = [(0, 5), (5, 10), (10, 14), (14, 18), (18, 22)]
    NRING = 8
    kb.reset(X1_END + 2048)
    gT = kb.sb("gT", [128, 5, L], BF16)
    wub = [kb.sb("wub%d" % i, [128, KC, 256], BF16, key="wub") for i in range(2)]
    assert kb.mark() <= P1_KEEP
    kb.reset(P3_KEEP)
    wdb = kb.sb("wdb", [128, NRING, D], BF16)
    asb = [kb.sb("asb%d" % i, [128, L + 2], F32, key="asb") for i in range(2)]
    facc = [kb.sb("facc%d" % i, [128, 512], F32, key="facc") for i in range(2)]
    fth = [kb.sb("fth%d" % i, [128, 512], F32, key="fth") for i in range(2)]
    sqj4 = kb.sb("sqj4", [128, D], BF16)
    for i in range(2):
        kb.op("pool", lambda e, i=i: e.memset(asb[i][:, 0:1], 0.0), writes=[("asb", i, "h0")])
        kb.op("pool", lambda e, i=i: e.memset(asb[i][:, L + 1:L + 2], 0.0), writes=[("asb", i, "h1")])
    out_toks = []
    cntf = [0]

    def load_w(cgi):
        i2 = cgi % 2
        kb.dma("pool", wub[i2][:, :, 0:128], wup_v[:, :, cgi * 128:(cgi + 1) * 128], writes=[("wub", i2, 0)])
        kb.dma("pool", wub[i2][:, :, 128:256], wup_v[:, :, DFF + cgi * 128:DFF + (cgi + 1) * 128], writes=[("wub", i2, 1)])
        kb.dma("pool", wdb[:, cgi % NRING, :], wdn_d[cgi * 128:(cgi + 1) * 128, :], writes=[("wdb", cgi % NRING)])
        kb.op("pool", lambda e: e.tensor_tensor(wdb[:, cgi % NRING, :], wdb[:, cgi % NRING, :], gfbc[:], ALU.mult),
              reads=[("wdb", cgi % NRING), ("gfbc", 0), ("gfbc", 1)], writes=[("wdb", cgi % NRING)])

    def ffn_cg(cgi, cl):
        i2 = cgi % 2
        a = asb[i2]
        for tg in range(4):
            b = kb.bank()

            def mm(e, b=b, tg=tg):
                ins = None
                for k in range(KC):
                    ins = e.matmul(ps[b][:, :], wub[i2][:, k, 0:128], actT[:, k, tg * 512:(tg + 1) * 512], start=(k == 0), stop=(k == KC - 1))
                return ins
            kb.op("pe", mm, reads=[("wub", i2, 0)] + [k_ for t in range(tg * 4, tg * 4 + 4) for k_ in AT(t)], writes=[PS(b)])
            kb.op("act", lambda e, b=b, tg=tg: e.copy(a[:, 1 + tg * 512:1 + (tg + 1) * 512], ps[b][:, :]), reads=[PS(b)], writes=[("asb", i2, tg)])
        akeys = [("asb", i2, x_) for x_ in (0, 1, 2, 3, "h0", "h1")]

        def piece(tg):
            p0 = tg * 512
            j2 = cntf[0] % 2
            cntf[0] += 1
            fa, ft = facc[j2], fth[j2]
            bv = kb.bank()

            def mmv(e):
                ins = None
                for k in range(KC):
                    ins = e.matmul(ps[bv][:, :], wub[i2][:, k, 128:256], actT[:, k, p0:p0 + 512], start=(k == 0), stop=(k == KC - 1))
                return ins
            kb.op("pe", mmv, reads=[("wub", i2, 1)] + [k_ for t in range(tg * 4, tg * 4 + 4) for k_ in AT(t)], writes=[PS(bv)])
            kb.op("act", lambda e: e.activation(fa[:], a[:, 1 + p0:1 + p0 + 512], AF.Identity, bias=ffcw[:, cgi, 3:4], scale=ffcw[:, cgi, 1:2]),
                  reads=akeys + ["ffcw"], writes=[("facc", j2)])
            kb.op("dve", lambda e: e.scalar_tensor_tensor(fa[:], a[:, p0:p0 + 512], ffcw[:, cgi, 0:1], fa[:], ALU.mult, ALU.add),
                  reads=akeys + ["ffcw", ("facc", j2)], writes=[("facc", j2)])
            kb.op("dve", lambda e: e.scalar_tensor_tensor(fa[:], a[:, 2 + p0:2 + p0 + 512], ffcw[:, cgi, 2:3], fa[:], ALU.mult, ALU.add),
                  reads=akeys + ["ffcw", ("facc", j2)], writes=[("facc", j2)])
            kb.op("act", lambda e: e.activation(ft[:], fa[:], AF.Gelu_apprx_tanh), reads=[("facc", j2)], writes=[("fth", j2)])
            kb.op("dve", lambda e: e.tensor_tensor(gT[:, cl, p0:p0 + 512], ft[:], ps[bv][:, :], ALU.mult),
                  reads=[("fth", j2), PS(bv)], writes=[("gT", cl, tg)])
        for tg in range(4):
            piece(tg)

    def down(gi, c0, c1, T, last):
        i2 = T % 2
        for half in range(2):
            b = kb.bank()

            def mm(e, b=b, half=half):
                ins = None
                for cgi in range(c0, c1):
                    ins = e.matmul(ps[b][:, :], gT[:, cgi - c0, T * 128:(T + 1) * 128], wdb[:, cgi % NRING, half * 512:(half + 1) * 512],
                                   start=(cgi == c0), stop=(cgi == c1 - 1))
                return ins
            kb.op("pe", mm, reads=[("gT", cgi - c0, T // 4) for cgi in range(c0, c1)] + [("wdb", cgi % NRING) for cgi in range(c0, c1)],
                  writes=[PS(b)])
            kb.op("dve", lambda e, b=b, half=half: e.tensor_tensor(x1[:, T, half * 512:(half + 1) * 512], ps[b][:, :],
                                                                   x1[:, T, half * 512:(half + 1) * 512], ALU.add),
                  reads=[PS(b), ("x1", T, half)], writes=[("x1", T, half)])
        if last:
            si = NTT + NT + T
            kb.op("act", lambda e: e.activation(sqj4[:], x1[:, T, :], AF.Square, accum_out=stat[:, 0, si:si + 1]),
                  reads=[("x1", T, 0), ("x1", T, 1)], writes=["sqj4", ("stat0", si)])
            rstd_ops(si)
            def fin(T=T, si=si):
                kb.op("dve", lambda e: e.scalar_tensor_tensor(x1[:, T, :], x1[:, T, :], stat[:, 2, si:si + 1], fnb[:], ALU.mult, ALU.mult),
                      reads=[("x1", T, 0), ("x1", T, 1), ("stat2", si), "fnb"], writes=[("x1", T, 0), ("x1", T, 1)])
                out_toks.append(kb.dma("sp", out_v[T], x1[:, T, :], reads=[("x1", T, 0), ("x1", T, 1)]))
            pend_fin.append(fin)
            while len(pend_fin) > 1:
                pend_fin.pop(0)()

    pend_fin = []

    load_w(0)
    for gi, (c0, c1) in enumerate(GROUPS):
        for cgi in range(c0, c1):
            if cgi + 1 < NCG:
                load_w(cgi + 1)
            ffn_cg(cgi, cgi - c0)
        for T in range(NT):
            down(gi, c0, c1, T, gi == len(GROUPS) - 1)
    while pend_fin:
        pend_fin.pop(0)()
    kb.dumps.extend(out_toks)
    return finish(kb)


def finish(kb):
    toks = list(kb.dumps)
    kb.wait_all("sp", toks)
    return kb.build()


def _fm(v):
    return np.ascontiguousarray(np.asarray(v, np.float32).reshape(-1, 128).T)


def _bcast(v, n=128):
    v = np.asarray(v, np.float32)
    return np.ascontiguousarray(np.broadcast_to(v[None], (n,) + v.shape))


def _rope_table():
    tok = np.arange(L)
    row = (tok // 64).astype(np.float64)
    col = (tok % 64).astype(np.float64)
    inv = 10000.0 ** (-np.arange(16, dtype=np.float64) / 16)
    tab = np.zeros((L, 2, 64), np.float64)
    for d in range(64):
        axis, half, p = d // 32, (d % 32) // 16, d % 16
        ang = (row if axis == 0 else col) * inv[p]
        tab[:, 0, d] = np.cos(ang)
        tab[:, 1, d] = -np.sin(ang) if half == 0 else np.sin(ang)
    return np.ascontiguousarray(tab.reshape(NT, 128, 2, 64).transpose(1, 0, 2, 3)).astype(np.float32)


def _consts():
    s = np.arange(128)[:, None]
    t = np.arange(128)[None, :]
    c = np.zeros((128, 6, 128), np.float32)
    c[:, 0] = (s == t)
    c[:, 1] = (s <= t)
    c[:, 2] = (s >= t)
    c[:, 3] = 1.0
    c[:, 4] = np.where(s >= t, 0.0, NEG)
    c[:, 5] = np.where(s <= t, 0.0, NEG)
    return c


_QH = np.concatenate([2064 + 64 * h + np.arange(64) for h in (0, 4, 1, 5, 2, 6, 3, 7)])
_PERM = np.concatenate([np.arange(0, 2048), _QH, np.arange(2576, 2832), np.arange(2048, 2064)])


def host_maps(x, c, ctx, c_ctx, w_ada, b_ada, norm_mix, norm_ffn, w_in, gate_b, qk_conv_w, qk_conv_b,
              mlstm_norm, attn_sink, w_out, w_up, ffn_conv_w, ffn_conv_b, w_down, final_norm, cores=range(8)):
    f = lambda a: np.ascontiguousarray(np.asarray(a, np.float32))
    b_ada0 = f(b_ada)[0]
    shared = {
        "bada_fm": _fm(b_ada0),
        "bada_g": _bcast(np.stack([b_ada0[2048:3072], b_ada0[5120:6144]])),
        "qkcw": np.ascontiguousarray(np.stack([_fm(f(qk_conv_w)[0, 0]), _fm(f(qk_conv_w)[0, 1]), _fm(f(qk_conv_w)[0, 2]),
                                               _fm(f(qk_conv_b)[0])], axis=-1)),
        "ffcw": np.ascontiguousarray(np.stack([_fm(f(ffn_conv_w)[0, 0]), _fm(f(ffn_conv_w)[0, 1]), _fm(f(ffn_conv_w)[0, 2]),
                                               _fm(f(ffn_conv_b)[0])], axis=-1)),
        "gate_b": _bcast(f(gate_b)[0]),
        "gain": _bcast(f(mlstm_norm)[0]),
        "sink": _bcast(f(attn_sink)[0]),
        "fnorm": _bcast(f(final_norm)),
        "rope": _rope_table(),
        "consts": _consts(),
        "w_ada": f(w_ada)[0],
        "w_in": np.ascontiguousarray(f(w_in)[0][:, _PERM]),
        "w_out": f(w_out)[0],
        "w_up": f(w_up)[0],
        "w_down": f(w_down)[0],
    }
    maps = []
    xs, cs, ctxs = f(x), f(c), f(ctx)
    for b in cores:
        m = dict(shared)
        m["x"] = xs[b]
        m["ctx"] = ctxs[b]
        m["vecs"] = np.ascontiguousarray(np.stack([_fm(cs[b]), _fm(f(c_ctx)), _fm(f(norm_mix)[0]), _fm(f(norm_ffn)[0])], axis=1))
        maps.append(m)
    return maps


_NC_CACHE = {}


def kernel(**inputs):
    if "nc" not in _NC_CACHE:
        _NC_CACHE["nc"] = build()
    nc = _NC_CACHE["nc"]
    maps = host_maps(**inputs)
    res = run_bass_kernel_spmd(nc, maps, core_ids=list(range(8)))
    return np.stack([np.asarray(r["out"], np.float32) for r in res.results], axis=0)
```

```python
from contextlib import ExitStack
import numpy as np
import concourse.bass as bass
import concourse.mybir as mybir
from concourse.bass_utils import run_bass_kernel_spmd

F32 = mybir.dt.float32
BF16 = mybir.dt.bfloat16
U8 = mybir.dt.uint8
AF = mybir.ActivationFunctionType
ALU = mybir.AluOpType
AX = mybir.AxisListType

ENGS = ("pe", "act", "dve", "pool", "sp")
DT_SIZE = {F32: 4, BF16: 2}

import os as _os
F_SILU = _os.environ.get("K_SILU", "1") == "1"
F_ARSTD = _os.environ.get("K_ARSTD", "1") == "1"
F_INTER = _os.environ.get("K_INTER", "1") == "1"

D = 1024
L = 2048
NT = 16
CTXL = 256
NCT = 2
NTT = NT + NCT
KC = 8
DFF = 2816
NCG = 22
INC = 2832
EPS = 1e-6
KAPPA = 1.0 / np.sqrt(128.0)
LNK = float(np.log(KAPPA if F_SILU else KAPPA * 0.25))
NEG = -30000.0
ARENA = 207 * 1024


def _name(k):
    return k[0] if isinstance(k, tuple) else k


class KB:
    def __init__(self, n_dma_slots=16):
        self.nc = bass.Bass("TRN2", target_bir_lowering=False)
        self.es = ExitStack()
        self.ops = {e: [] for e in ENGS}
        self.seq = {e: 0 for e in ENGS}
        self.sems = {}
        for e in ENGS:
            self.sems[e] = self.es.enter_context(self.nc.semaphore("s_" + e))
        self.nslots = {"sp": n_dma_slots, "act": 2, "pool": n_dma_slots}
        self.dma_slot_next = {}
        self.dma_slot_val = {}
        for q in ("sp", "act", "pool"):
            for i in range(self.nslots[q]):
                key = ("dma", q, i)
                self.sems[key] = self.es.enter_context(self.nc.semaphore("d_%s%d" % (q, i)))
                self.dma_slot_val[key] = 0
            self.dma_slot_next[q] = 0
        self.last_w = {}
        self.readers = {}
        self.waited = {e: {} for e in ENGS}
        self.keys_by_name = {}
        self.arena = self.nc.alloc_sbuf_tensor("arena", [128, ARENA], U8)
        self.abase = self.nc.lookup_mloc(self.arena).addr
        self.atop = 0
        self.ahigh = 0
        self.live = []
        self.alias = {}
        self.alias_sum = {}
        self.nbank = 0
        self.dumps = []

    def sb(self, name, shape, dt, key=None, at=None):
        key = key or name
        size = int(np.prod(shape[1:])) * DT_SIZE[dt]
        size = (size + 63) // 64 * 64
        if at is not None:
            start = at
        else:
            start = self.atop
            self.atop += size
            self.ahigh = max(self.ahigh, self.atop)
            assert self.atop <= ARENA, (name, self.atop)
        old = [n for (n, s, e) in self.live if s < start + size and e > start and n != key]
        if old:
            self.alias.setdefault(key, set()).update(old)
        self.live.append((key, start, start + size))
        return self.nc.alloc_sbuf_tensor_at(name, list(shape), dt, offset=self.abase + start)

    def mark(self):
        return self.atop

    def reset(self, m):
        self.atop = m

    def ps(self, name, shape, dt):
        return self.es.enter_context(self.nc.psum_tensor(name, list(shape), dt))

    def dram(self, name, shape, dt, kind):
        return self.nc.dram_tensor(name, list(shape), dt, kind=kind)

    def _alias_deps(self, name):
        s = self.alias_sum.get(name)
        if s is None:
            s = {}
            for on in self.alias[name]:
                for k in self.keys_by_name.get(on, ()):
                    toks = list(self.readers.get(k, ()))
                    t = self.last_w.get(k)
                    if t is not None:
                        toks.append(t)
                    for (sk, v) in toks:
                        if s.get(sk, 0) < v:
                            s[sk] = v
            self.alias_sum[name] = s
        return s.items()

    def _deps(self, eng, reads, writes, is_dma=False, psr=()):
        deps = []
        for r in psr:
            for t in self.readers.get(r, ()):
                if t[0] != eng:
                    deps.append(t)
        for r in reads:
            t = self.last_w.get(r)
            if t is not None:
                deps.append(t)
            n = _name(r)
            if n in self.alias:
                deps.extend(self._alias_deps(n))
        for w in writes:
            t = self.last_w.get(w)
            if t is not None:
                deps.append(t)
            for t in self.readers.get(w, ()):
                deps.append(t)
            n = _name(w)
            if n in self.alias:
                deps.extend(self._alias_deps(n))
        wd = self.waited[eng]
        best = {}
        for (k, v) in deps:
            if wd.get(k, 0) >= v:
                continue
            if best.get(k, 0) < v:
                best[k] = v
        waits = []
        for k, v in best.items():
            wd[k] = v
            waits.append((k, v))
        return waits

    def _commit(self, tok, reads, writes):
        for r in reads:
            self.readers.setdefault(r, []).append(tok)
            self.keys_by_name.setdefault(_name(r), set()).add(r)
        for w in writes:
            self.last_w[w] = tok
            self.readers[w] = []
            self.keys_by_name.setdefault(_name(w), set()).add(w)

    def op(self, eng, fn, reads=(), writes=(), accum=False):
        reads = list(reads)
        writes = list(writes)
        psr = [r for r in reads if isinstance(r, tuple) and r[0] == "ps"]
        if accum:
            waits = self._deps(eng, reads, [], psr=psr)
        else:
            waits = self._deps(eng, reads, writes, psr=psr)
        self.seq[eng] += 1
        tok = (eng, self.seq[eng])
        self.ops[eng].append((waits, fn, (eng, 1)))
        self._commit(tok, reads, writes)
        return tok

    def dma(self, q, out, in_, reads=(), writes=(), **kw):
        reads = list(reads)
        writes = list(writes)
        i = self.dma_slot_next[q]
        self.dma_slot_next[q] = (i + 1) % self.nslots[q]
        key = ("dma", q, i)
        waits = self._deps(q, reads, writes, is_dma=True)
        pv = self.dma_slot_val[key]
        if pv > 0 and self.waited[q].get(key, 0) < pv:
            self.waited[q][key] = pv
            waits.append((key, pv))
        self.dma_slot_val[key] = pv + 16
        tok = (key, pv + 16)

        def fn(e, out=out, in_=in_, kw=kw):
            return e.dma_start(out=out, in_=in_, **kw)

        self.ops[q].append((waits, fn, (key, 16)))
        self._commit(tok, reads, writes)
        return tok

    def wait_all(self, eng, toks):
        waits = []
        for (k, v) in toks:
            if self.waited[eng].get(k, 0) < v:
                self.waited[eng][k] = v
                waits.append((k, v))
        self.ops[eng].append((waits, None, None))

    def bank(self):
        b = self.nbank
        self.nbank = (b + 1) % 8
        return b

    def dump(self, name, t, keys):
        d = self.dram("dbg_" + name, list(t.shape), t.dtype, "ExternalOutput")
        tok = self.dma("sp", d.ap(), t[:], reads=keys)
        self.dumps.append(tok)

    def build(self):
        nc = self.nc
        sems = self.sems
        ops = self.ops
        with nc.Block() as block:
            def emit(e, lst):
                for (waits, fn, inc) in lst:
                    for (k, v) in waits:
                        e.wait_ge(sems[k], v)
                    if fn is None:
                        continue
                    ins = fn(e)
                    ins.then_inc(sems[inc[0]], inc[1])

            @block.tensor
            def _(e):
                emit(e, ops["pe"])

            @block.scalar
            def _(e):
                emit(e, ops["act"])

            @block.vector
            def _(e):
                emit(e, ops["dve"])

            @block.gpsimd
            def _(e):
                emit(e, ops["pool"])

            @block.sync
            def _(e):
                emit(e, ops["sp"])
        self.es.close()
        return nc


def bc(ap_, shape):
    return ap_.to_broadcast(list(shape))


def build(stage=99, debug=False):
    kb = KB()
    nc = kb.nc
    inp = lambda n, s: kb.dram(n, s, F32, "ExternalInput").ap()
    x_d = inp("x", [L, D])
    ctx_d = inp("ctx", [CTXL, D])
    vecs_d = inp("vecs", [128, 4, 8])
    badafm_d = inp("bada_fm", [128, 48])
    badag_d = inp("bada_g", [128, 2, 1024])
    qkcw_d = inp("qkcw", [128, 8, 4])
    ffcw_d = inp("ffcw", [128, NCG, 4])
    gateb_d = inp("gate_b", [128, 16])
    gain_d = inp("gain", [128, 512])
    sink_d = inp("sink", [128, 8])
    fnorm_d = inp("fnorm", [128, D])
    rope_d = inp("rope", [128, NT, 2, 64])
    cst_d = inp("consts", [128, 6, 128])
    wada_d = inp("w_ada", [D, 6 * D])
    win_d = inp("w_in", [D, INC])
    wout_d = inp("w_out", [D, D])
    wup_d = inp("w_up", [D, 2 * DFF])
    wdn_d = inp("w_down", [DFF, D])
    out_d = kb.dram("out", [L, D], F32, "ExternalOutput").ap()

    wada_v = wada_d.rearrange("(k p) c -> p k c", p=128)
    win_v = win_d.rearrange("(k p) c -> p k c", p=128)
    wout_v = wout_d.rearrange("(k p) c -> p k c", p=128)
    wup_v = wup_d.rearrange("(k p) c -> p k c", p=128)
    wdn_v = wdn_d.rearrange("(k p) c -> p k c", p=128)
    x_v = x_d.rearrange("(t p) d -> t p d", p=128)
    ctx_v = ctx_d.rearrange("(t p) d -> t p d", p=128)
    out_v = out_d.rearrange("(t p) d -> t p d", p=128)

    ps = [kb.ps("ps%d" % b, [128, 512], F32) for b in range(8)]
    psb = [p[:].bitcast(BF16) for p in ps]

    def PS(b):
        return ("ps", b)

    def AT(t):
        return [("actT", t, "d"), ("actT", t, "a")]

    def HC(t):
        return [("hxTc", t, "d"), ("hxTc", t, "a")]

    cstf = kb.sb("cstf", [128, 6, 128], F32)
    identb = kb.sb("identb", [128, 128], BF16)
    mnegb = kb.sb("mnegb", [128, 2, 512], BF16)
    vecs = kb.sb("vecs", [128, 4, 8], F32)
    badafm = kb.sb("badafm", [128, 48], F32)
    qkcw = kb.sb("qkcw", [128, 8, 4], F32)
    ffcw = kb.sb("ffcw", [128, NCG, 4], F32)
    modfm = kb.sb("modfm", [128, 6, 8, 2], F32)
    scsh = kb.sb("scsh", [128, 6, 8], F32)
    sc2 = kb.sb("sc2", [128, 8, 2], BF16)
    silrep = kb.sb("silrep", [128, 8, 128], BF16)
    stat = kb.sb("stat", [128, 3, NTT + 2 * NT], F32)
    nhalf = kb.sb("nhalf", [128, 64], F32)
    epsb = kb.sb("epsb", [128, 8], F32)
    GAINH_AT = kb.mark()
    gainh = kb.sb("gainh", [128, 512], F32)
    esink = kb.sb("esink", [128, 8], F32)
    small = kb.sb("small", [128, 64], F32)
    actT = kb.sb("actT", [128, KC, L], BF16)
    P_END = kb.mark()
    tI, tU, tL, tO = 0, 1, 2, 3


    kb.dma("sp", cstf[:], cst_d, writes=["cstf"])
    kb.dma("sp", vecs[:], vecs_d, writes=["vecs"])
    kb.dma("sp", badafm[:], badafm_d, writes=["badafm"])
    kb.dma("sp", qkcw[:], qkcw_d, writes=["qkcw"])
    kb.dma("sp", ffcw[:], ffcw_d, writes=["ffcw"])
    kb.dma("sp", gainh[:], gain_d, writes=["gainh"])
    kb.dma("sp", esink[:], sink_d, writes=["esink"])
    kb.op("dve", lambda e: e.tensor_copy(identb[:], cstf[:, tI, :]), reads=["cstf"], writes=["identb"])
    kb.op("dve", lambda e: e.tensor_copy(mnegb[:, 0, :].rearrange("p (r c) -> p r c", r=4),
                                         cstf[:, tL, :].unsqueeze(1).to_broadcast([128, 4, 128])),
          reads=["cstf"], writes=["mnegb"])
    kb.op("dve", lambda e: e.tensor_copy(mnegb[:, 1, :].rearrange("p (r c) -> p r c", r=4),
                                         cstf[:, tU, :].unsqueeze(1).to_broadcast([128, 4, 128])),
          reads=["cstf"], writes=["mnegb"])
    kb.op("pool", lambda e: e.memset(nhalf[:], -0.5), writes=["nhalf"])
    kb.op("pool", lambda e: e.memset(epsb[:], EPS), writes=["epsb"])
    if debug:
        kb.op("pool", lambda e: e.memset(modfm[:], 0.0), writes=[("modfm", q) for q in range(6)])
    kb.op("act", lambda e: e.activation(esink[:], esink[:], AF.Exp), reads=["esink"], writes=["esink"])

    m_ph = kb.mark()
    wab = [kb.sb("wab%d" % i, [128, KC, 1024], BF16, key="wab") for i in range(2)]
    tmp0 = kb.sb("tmp0", [128, 2, 8], F32)
    tmp1 = kb.sb("tmp1", [128, 2, 8], F32)
    kb.op("act", lambda e: e.activation(tmp0[:], vecs[:, 0:2, :], AF.Tanh, scale=0.5), reads=["vecs"], writes=["tmp0"])
    kb.op("dve", lambda e: e.scalar_tensor_tensor(tmp1[:], tmp0[:], 1.0, vecs[:, 0:2, :], ALU.add, ALU.mult),
          reads=["tmp0", "vecs"], writes=["tmp1"])
    kb.op("dve", lambda e: e.tensor_scalar(sc2[:].rearrange("p k c -> p c k"), tmp1[:], 0.5, None, ALU.mult),
          reads=["tmp1"], writes=["sc2"])
    kb.op("dve", lambda e: e.tensor_copy(silrep[:], sc2[:, :, 0:1].to_broadcast([128, 8, 128])),
          reads=["sc2"], writes=["silrep"])

    def load_wada(q):
        buf = wab[q % 2]
        kb.dma("pool", buf[:], wada_v[:, :, q * 1024:(q + 1) * 1024], writes=[("wab", q % 2)])
        return buf

    def adaln_fm(q, buf, bkey=None):
        bkey = bkey or ("wab", q % 2)
        b = kb.bank()

        def mm(e):
            ins = None
            for j in range(8):
                for k in range(KC):
                    ins = e.matmul(ps[b][:, j * 2:(j + 1) * 2], buf[:, k, j * 128:(j + 1) * 128], sc2[:, k, :],
                                   start=(k == 0), stop=(k == KC - 1))
            return ins
        kb.op("pe", mm, reads=[bkey, "sc2"], writes=[PS(b)])
        kb.op("dve", lambda e: e.tensor_tensor(modfm[:, q, :, :], ps[b][:, 0:16].rearrange("p (j c) -> p j c", c=2),
                                               badafm[:, q * 8:(q + 1) * 8].unsqueeze(2).to_broadcast([128, 8, 2]), ALU.add),
              reads=[PS(b), "badafm"], writes=[("modfm", q)])

    wa_bufs = [load_wada(q) for q in (0, 1)]
    kb.op("pool", lambda e: e.tensor_scalar(gainh[:], gainh[:], 0.5, None, ALU.mult), reads=["gainh"], writes=["gainh"])
    for q in (0, 1):
        adaln_fm(q, wa_bufs[q])
    nm = vecs[:, 2, :]
    nf = vecs[:, 3, :]
    kb.op("dve", lambda e: e.scalar_tensor_tensor(scsh[:, 0, :], modfm[:, 1, :, 0], 1.0, nm, ALU.add, ALU.mult),
          reads=[("modfm", 1), "vecs"], writes=[("scsh", 0)])
    kb.op("dve", lambda e: e.tensor_copy(scsh[:, 1, :], modfm[:, 0, :, 0]), reads=[("modfm", 0)], writes=[("scsh", 1)])
    kb.op("dve", lambda e: e.scalar_tensor_tensor(scsh[:, 2, :], modfm[:, 1, :, 1], 1.0, nm, ALU.add, ALU.mult),
          reads=[("modfm", 1), "vecs"], writes=[("scsh", 2)])
    kb.op("dve", lambda e: e.tensor_copy(scsh[:, 3, :], modfm[:, 0, :, 1]), reads=[("modfm", 0)], writes=[("scsh", 3)])
    kb.reset(m_ph)

    qT = kb.sb("qT", [128, 4, L], BF16)
    kT = kb.sb("kT", [128, 4, L], BF16)
    kTc = kb.sb("kTc", [128, 4, CTXL], BF16)
    NLO = 8
    v1lo = kb.sb("v1lo", [128, NCT + NLO, 4, 129], BF16)
    oglo = kb.sb("oglo", [128, NLO, 512], BF16)
    qrlo = kb.sb("qrlo", [128, NLO, 512], BF16)
    kaT = kb.sb("kaT", [128, NTT * 128], BF16)
    va1 = kb.sb("va1", [128, NTT, 2, 65], BF16)
    gsb = kb.sb("gsb", [128, NTT, 16], F32)
    V1HI_AT = kb.mark()
    v1hi = kb.sb("v1hi", [128, NT - NLO, 4, 129], BF16)
    OGHI_AT = kb.mark()
    oghi = kb.sb("oghi", [128, NT - NLO, 512], BF16)
    QRHI_AT = kb.mark()
    qrhi = kb.sb("qrhi", [128, NT - NLO, 512], BF16)

    def V1(ti):
        return v1lo[:, ti, :, :] if ti < NCT + NLO else v1hi[:, ti - NCT - NLO, :, :]

    def V1K(ti):
        return ("v1lo", ti) if ti < NCT + NLO else ("v1hi", ti)

    def V1O(ti):
        return ("v1lo", "ones") if ti < NCT + NLO else ("v1hi", "ones")

    def OG(t):
        return oglo[:, t, :] if t < NLO else oghi[:, t - NLO, :]

    def OGK(t):
        return ("oglo", t) if t < NLO else ("oghi", t)

    def QR(t):
        return qrlo[:, t, :] if t < NLO else qrhi[:, t - NLO, :]

    def QRK(t):
        return ("qrlo", t) if t < NLO else ("qrhi", t)

    P1_KEEP = kb.mark()
    usb = [kb.sb("usb%d" % i, [128, L + 2], F32, key="usb") for i in range(2)]
    usbc = [kb.sb("usbc%d" % i, [128, CTXL + 2], F32, key="usbc") for i in range(2)]
    cacc = [kb.sb("cacc%d" % i, [128, 512], F32, key="cacc") for i in range(2)]
    cth = [kb.sb("cth%d" % i, [128, 512], F32, key="cth") for i in range(2)]
    krb = [kb.sb("krb%d" % i, [128, 128], BF16, key="krb") for i in range(2)]
    hxTc = kb.sb("hxTc", [128, KC, CTXL], BF16)
    kb.atop = max(kb.atop, P1_KEEP + 32256)
    A_END = kb.mark()
    ropet2 = kb.sb("ropet2", [128, NT, 2, 64], F32)
    gbb = kb.sb("gbb", [128, 16], F32)
    kb.dma("sp", ropet2[:], rope_d, writes=["ropet2"])
    kb.dma("sp", gbb[:], gateb_d, writes=["gbb"])
    P1_END = kb.mark()

    NXB = 3
    xbuf = [kb.sb("xbuf%d" % i, [128, D], F32, key="xbuf") for i in range(NXB)]
    xnb = [kb.sb("xnb%d" % i, [128, D], BF16, key="xnb") for i in range(NXB)]
    sqj = kb.sb("sqj", [128, D], BF16)
    kb.op("pool", lambda e: e.memset(v1lo[:, :, :, 128:129], 1.0), writes=[("v1lo", "ones")])
    kb.op("pool", lambda e: e.memset(v1hi[:, :, :, 128:129], 1.0), writes=[("v1hi", "ones")])
    kb.op("pool", lambda e: e.memset(va1[:, :, :, 64:65], 1.0), writes=[("va1", "ones")])

    def rstd_ops(si):
        if F_ARSTD:
            kb.op("act", lambda e: e.activation(stat[:, 1, si:si + 1], stat[:, 0, si:si + 1], AF.Ln, scale=1.0 / D, bias=epsb[:, 0:1]),
                  reads=[("stat0", si), "epsb"], writes=[("stat1", si)])
            kb.op("act", lambda e: e.activation(stat[:, 2, si:si + 1], stat[:, 1, si:si + 1], AF.Exp, scale=-0.5),
                  reads=[("stat1", si)], writes=[("stat2", si)])
        else:
            kb.op("pool", lambda e: e.tensor_scalar(stat[:, 1, si:si + 1], stat[:, 0, si:si + 1], 1.0 / D, EPS, ALU.mult, ALU.add),
                  reads=[("stat0", si)], writes=[("stat1", si)])
            kb.op("pool", lambda e: e.tensor_tensor(stat[:, 2, si:si + 1], stat[:, 1, si:si + 1], nhalf[:, 0:1], ALU.pow),
                  reads=[("stat1", si), "nhalf"], writes=[("stat2", si)])

    def norm_tile(src_ap, si, xb, sc_i, sh_i, dst, dst_tok0, dst_key):
        i2 = si % NXB
        kb.dma("sp", xbuf[i2][:], src_ap, writes=[("xbuf", i2)])
        kb.op("act", lambda e: e.activation(sqj[:], xbuf[i2][:], AF.Square, accum_out=stat[:, 0, si:si + 1]),
              reads=[("xbuf", i2)], writes=["sqj", ("stat0", si)])
        rstd_ops(si)

        def rest():
            kb.op("dve", lambda e: e.tensor_scalar(xnb[i2][:], xbuf[i2][:], stat[:, 2, si:si + 1], None, ALU.mult),
                  reads=[("xbuf", i2), ("stat2", si)], writes=[("xnb", i2)])
            tr_evac(xnb[i2], ("xnb", i2), sc_i, sh_i, dst, dst_tok0, dst_key)
        return rest

    NDV = 6

    def tr_evac(xn, xkey, sc_i, sh_i, dst, dst_tok0, dst_key, NDV=6):
        ba = kb.bank()
        bb = kb.bank()

        def tr(e):
            ins = None
            for c in range(KC):
                o = psb[ba][:, c * 128:(c + 1) * 128] if c < NDV else psb[bb][:, (c - NDV) * 128:(c - NDV + 1) * 128]
                ins = e.transpose(o, xn[:, c * 128:(c + 1) * 128], identb[:])
            return ins
        kb.op("pe", tr, reads=[xkey, "identb"], writes=[PS(ba), PS(bb)])
        pend_ev.append(lambda: evac_part(ba, bb, sc_i, sh_i, dst, dst_tok0, dst_key, NDV))
        if len(pend_ev) > 1:
            pend_ev.pop(0)()

    pend_ev = []

    def flush_ev():
        while pend_ev:
            pend_ev.pop(0)()

    def evac_part(ba, bb, sc_i, sh_i, dst, dst_tok0, dst_key, NDV):
        for c in range(KC):
            if c < NDV:
                kb.op("dve", lambda e, c=c: e.tensor_scalar(dst[:, c, dst_tok0:dst_tok0 + 128], psb[ba][:, c * 128:(c + 1) * 128],
                                                            scsh[:, sc_i, c:c + 1], scsh[:, sh_i, c:c + 1], ALU.mult, ALU.add),
                      reads=[PS(ba), ("scsh", sc_i), ("scsh", sh_i)], writes=[dst_key + ("d",)])
            else:
                kb.op("act", lambda e, c=c: e.activation(dst[:, c, dst_tok0:dst_tok0 + 128], psb[bb][:, (c - NDV) * 128:(c - NDV + 1) * 128],
                                                         AF.Identity, bias=scsh[:, sh_i, c:c + 1], scale=scsh[:, sc_i, c:c + 1]),
                      reads=[PS(bb), ("scsh", sc_i), ("scsh", sh_i)], writes=[dst_key + ("a",)])

    pend_n = []
    for ci in range(NCT):
        pend_n.append(norm_tile(ctx_v[ci], ci, None, 2, 3, hxTc, ci * 128, ("hxTc", ci)))
    for t in range(NT):
        pend_n.append(norm_tile(x_v[t], NCT + t, None, 0, 1, actT, t * 128, ("actT", t)))
        if len(pend_n) >= NXB:
            pend_n.pop(0)()
    while pend_n:
        pend_n.pop(0)()
    flush_ev()
    if debug:
        kb.dump("hxT", actT, [k_ for t in range(NT) for k_ in AT(t)])
        kb.dump("hxTc", hxTc, [k_ for t in range(NCT) for k_ in HC(t)])
    if stage <= 1:
        return finish(kb)

    kb.reset(P1_END)
    wib = [kb.sb("wib%d" % i, [128, KC, 512], BF16, key="wib") for i in range(2)]
    rt1 = [kb.sb("rt1_%d" % i, [128, 512], F32, key="rt1") for i in range(2)]
    rt2 = [kb.sb("rt2_%d" % i, [128, 512], F32, key="rt2") for i in range(2)]
    for i in range(2):
        kb.op("pool", lambda e, i=i: e.memset(usb[i][:, 0:1], 0.0), writes=[("usb", i, "h0")])
        kb.op("pool", lambda e, i=i: e.memset(usb[i][:, L + 1:L + 2], 0.0), writes=[("usb", i, "h1")])
    for i in range(2):
        kb.op("pool", lambda e, i=i: e.memset(usbc[i][:, 0:1], 0.0), writes=[("usbc", i, "h0")])
        kb.op("pool", lambda e, i=i: e.memset(usbc[i][:, CTXL + 1:CTXL + 2], 0.0), writes=[("usbc", i, "h1")])

    WIN = {0: (0, 512), 1: (512, 512), 2: (2560, 272), 3: (1024, 512), 4: (1536, 512), 5: (2048, 512)}
    win_loaded = set()

    def prefetch_win(g):
        if g in WIN and g not in win_loaded:
            win_loaded.add(g)
            load_win(g, *WIN[g])

    def load_win(g, c0, ncols):
        buf = wib[g % 2]
        kb.dma("pool", buf[:, :, 0:ncols], win_v[:, :, c0:c0 + ncols], writes=[("wib", g % 2)])
        return buf

    cnt = [0]

    ring = [(cacc[0], ("cacc", 0)), (cacc[1], ("cacc", 1)), (cth[0], ("cth", 0)), (cth[1], ("cth", 1))]
    pend_silu = []

    def conv_silu(u, n_tok, j, dst_fn, dst_key_fn):
        def piece(p0):
            n = min(512, n_tok - p0)
            acc, akey = ring[cnt[0] % 4]
            cnt[0] += 1
            ukey = u[1]
            ut = u[0]
            kb.op("pool", lambda e: e.tensor_scalar(acc[:, 0:n], ut[:, 1 + p0:1 + p0 + n], qkcw[:, j, 1:2], qkcw[:, j, 3:4],
                                                    ALU.mult, ALU.add),
                  reads=ukey + ["qkcw"], writes=[akey])
            while len(pend_silu) > 1:
                pend_silu.pop(0)()
            kb.op("dve", lambda e: e.scalar_tensor_tensor(acc[:, 0:n], ut[:, p0:p0 + n], qkcw[:, j, 0:1], acc[:, 0:n], ALU.mult, ALU.add),
                  reads=ukey + ["qkcw", akey], writes=[akey])
            kb.op("dve", lambda e: e.scalar_tensor_tensor(acc[:, 0:n], ut[:, 2 + p0:2 + p0 + n], qkcw[:, j, 2:3], acc[:, 0:n], ALU.mult, ALU.add),
                  reads=ukey + ["qkcw", akey], writes=[akey])
            dst = dst_fn(p0, n)
            dkey = dst_key_fn(p0)
            pend_silu.append(lambda: kb.op("act", lambda e: e.activation(dst, acc[:, 0:n], AF.Silu), reads=[akey], writes=[dkey]))
        for p0 in range(0, n_tok, 512):
            piece(p0)

    pend_conv = []

    def fm_chunk(g, jj, buf):
        j = g * 4 + jj
        u2 = j % 2
        for tg in range(4):
            b = kb.bank()

            def mm(e, b=b, tg=tg):
                ins = None
                for k in range(KC):
                    ins = e.matmul(ps[b][:, :], buf[:, k, jj * 128:(jj + 1) * 128], actT[:, k, tg * 512:(tg + 1) * 512],
                                   start=(k == 0), stop=(k == KC - 1))
                return ins
            kb.op("pe", mm, reads=[("wib", g % 2)] + [k_ for t in range(tg * 4, tg * 4 + 4) for k_ in AT(t)], writes=[PS(b)])
            kb.op("act", lambda e, b=b, tg=tg: e.copy(usb[u2][:, 1 + tg * 512:1 + (tg + 1) * 512], ps[b][:, :]),
                  reads=[PS(b)], writes=[("usb", u2, tg)])
        ukeys = [("usb", u2, x_) for x_ in (0, 1, 2, 3, "h0", "h1")]
        if g == 1:
            b = kb.bank()

            def mmc(e, b=b):
                ins = None
                for k in range(KC):
                    ins = e.matmul(ps[b][:, 0:CTXL], buf[:, k, jj * 128:(jj + 1) * 128], hxTc[:, k, :],
                                   start=(k == 0), stop=(k == KC - 1))
                return ins
            kb.op("pe", mmc, reads=[("wib", g % 2)] + HC(0) + HC(1), writes=[PS(b)])
            kb.op("act", lambda e, b=b: e.copy(usbc[u2][:, 1:1 + CTXL], ps[b][:, 0:CTXL]), reads=[PS(b)], writes=[("usbc", u2)])
        while pend_conv:
            pend_conv.pop(0)()

        def conv():
            if g == 0:
                conv_silu((usb[u2], ukeys), L, j, lambda p0, n: qT[:, jj, p0:p0 + n], lambda p0: ("qT", jj, p0 // 512))
            else:
                conv_silu((usb[u2], ukeys), L, j, lambda p0, n: kT[:, jj, p0:p0 + n], lambda p0: ("kT", jj, p0 // 512))
                conv_silu((usbc[u2], [("usbc", u2), ("usbc", u2, "h0"), ("usbc", u2, "h1")]), CTXL, j,
                          lambda p0, n: kTc[:, jj, p0:p0 + n], lambda p0: ("kTc", jj))
        pend_conv.append(conv)

    for g in range(2):
        prefetch_win(g)
        buf = wib[g % 2]
        if g == 0:
            prefetch_win(1)
        for jj in range(4):
            fm_chunk(g, jj, buf)
            if g == 1 and jj == 0:
                prefetch_win(2)
    while pend_conv:
        pend_conv.pop(0)()
    while pend_silu:
        pend_silu.pop(0)()
    if debug:
        kb.dump("qT", qT, [("qT", jj, p) for jj in range(4) for p in range(4)])
        kb.dump("kT", kT, [("kT", jj, p) for jj in range(4) for p in range(4)])
        kb.dump("kTc", kTc, [("kTc", jj) for jj in range(4)])

    def hx_tile(ti):
        if ti < NCT:
            return (lambda k: hxTc[:, k, ti * 128:(ti + 1) * 128]), HC(ti)
        t = ti - NCT
        return (lambda k: actT[:, k, t * 128:(t + 1) * 128]), AT(t)

    def tm_group(g, c0, ncols, tiles, evac, hook=None):
        assert WIN[g] == (c0, ncols)
        prefetch_win(g)
        buf = wib[g % 2]
        for n_, ti in enumerate(tiles):
            if n_ == 2:
                prefetch_win(g + 1)
            lhs, hkey = hx_tile(ti)
            b = kb.bank()

            def mm(e, b=b, lhs=lhs, buf=buf):
                ins = None
                for k in range(KC):
                    ins = e.matmul(ps[b][:, 0:ncols], lhs(k), buf[:, k, 0:ncols], start=(k == 0), stop=(k == KC - 1))
                return ins
            kb.op("pe", mm, reads=[("wib", g % 2)] + hkey, writes=[PS(b)])
            evac(ti, b)
            if hook is not None:
                hook()

    def ev_v(ti, b):
        kb.op("act", lambda e: e.copy(V1(ti)[:, :, 0:128], ps[b][:, :].rearrange("p (h d) -> p h d", h=4)),
              reads=[PS(b)], writes=[V1K(ti)])

    def ev_o(ti, b):
        t = ti - NCT
        i2 = t % 2
        kb.op("act", lambda e: e.activation(rt1[i2][:], ps[b][:, :], AF.Tanh, scale=0.5), reads=[PS(b)], writes=[("rt1", i2)])
        kb.op("dve", lambda e: e.scalar_tensor_tensor(OG(t), rt1[i2][:], 1.0, gainh[:], ALU.add, ALU.mult),
              reads=[("rt1", i2), "gainh"], writes=[OGK(t)])

    def rope(src, nh, t, dst, skey, dkey, i2):
        cosb = ropet2[:, t, 0, :].unsqueeze(1).to_broadcast([128, nh, 64])
        s4 = src.rearrange("p (h a f q) -> p h a f q", h=nh, a=2, f=2)
        t1, t2 = rt1[i2], rt2[i2]
        t14 = t1[:, 0:nh * 64].rearrange("p (h a f q) -> p h a f q", h=nh, a=2, f=2)
        t24 = t2[:, 0:nh * 64].rearrange("p (h a f q) -> p h a f q", h=nh, a=2, f=2)
        d4 = dst.rearrange("p (h a f q) -> p h a f q", h=nh, a=2, f=2)
        sin4 = ropet2[:, t, 1, :].rearrange("p (a f q) -> p a f q", a=2, f=2)
        kb.op("dve", lambda e: e.tensor_tensor(t1[:, 0:nh * 64].rearrange("p (h d) -> p h d", h=nh),
                                               src.rearrange("p (h d) -> p h d", h=nh), cosb, ALU.mult),
              reads=[skey, "ropet2"], writes=[("rt1", i2)])
        for f in range(2):
            sb_ = sin4[:, :, f, :].unsqueeze(1).to_broadcast([128, nh, 2, 16])
            kb.op("dve", lambda e, f=f, sb_=sb_: e.tensor_tensor(t24[:, :, :, f, :], s4[:, :, :, 1 - f, :], sb_, ALU.mult),
                  reads=[skey, "ropet2"], writes=[("rt2", i2, f)])
        kb.op("pool", lambda e: e.tensor_tensor(dst, t1[:, 0:nh * 64], t2[:, 0:nh * 64], ALU.add),
              reads=[("rt1", i2), ("rt2", i2, 0), ("rt2", i2, 1)], writes=[dkey])

    def ev_q(ti, b):
        t = ti - NCT
        rope(ps[b][:, :], 8, t, QR(t), PS(b), QRK(t), t % 2)

    def ev_kvg(ti, b):
        i2 = ti % 2
        if ti < NCT:
            kb.op("act", lambda e: e.copy(krb[i2][:], ps[b][:, 0:128]), reads=[PS(b)], writes=[("krb", i2)])
        else:
            rope(ps[b][:, 0:128], 2, ti - NCT, krb[i2][:], PS(b), ("krb", i2), i2)
        kb.op("act", lambda e: e.copy(va1[:, ti, :, 0:64], ps[b][:, 128:256].rearrange("p (h d) -> p h d", h=2)),
              reads=[PS(b)], writes=[("va1", ti)])
        kb.op("dve", lambda e: e.tensor_tensor(gsb[:, ti, :], ps[b][:, 256:272], gbb[:], ALU.add),
              reads=[PS(b), "gbb"], writes=[("gsb", ti)])
        def do_tr():
            b2 = kb.bank()
            kb.op("pe", lambda e: e.transpose(psb[b2][:, 0:128], krb[i2][:], identb[:]), reads=[("krb", i2), "identb"], writes=[PS(b2)])
            kb.op("dve", lambda e: e.tensor_copy(kaT[:, ti * 128:(ti + 1) * 128], psb[b2][:, 0:128]),
                  reads=[PS(b2)], writes=[("kaT", ti)])
        if pend_tr:
            pend_tr.pop(0)()
        pend_tr.append(do_tr)

    pend_tr = []
    tm_group(2, 2560, 272, range(NTT), ev_kvg)
    while pend_tr:
        pend_tr.pop(0)()
    tm_group(3, 1024, 512, range(NTT), ev_v)
    kb.reset(P1_KEEP)
    lfs = kb.sb("lfs", [128, NTT, 2, 4], F32)
    gd = kb.sb("gd", [128, 3, NTT, 2, 4], F32)
    tmpc = kb.sb("tmpc", [128, NTT, 2, 4], F32)
    Cst = kb.sb("Cst", [128, 2, 4, 129], F32)
    Cbf = kb.sb("Cbf", [128, 2, 4, 129], BF16)
    C0lo = kb.sb("C0lo", [128, NLO, 4, 129], BF16)
    C0HI_AT = kb.mark()
    C0hi = kb.sb("C0hi", [128, NT - NLO, 4, 129], BF16)

    def C0(t):
        return C0lo[:, t, :, :] if t < NLO else C0hi[:, t - NLO, :, :]

    def C0K(t, p_):
        return ("C0lo", t, p_) if t < NLO else ("C0hi", t, p_)
    ktm = [kb.sb("ktm%d" % i, [128, 4, 128], BF16, key="ktm") for i in range(2)]
    vwb = [[kb.sb("vwb%d_%d" % (d_, i), [128, 4, 129], BF16, key="vwb") for i in range(2)] for d_ in range(2)]
    gview = gsb[:, :, :].rearrange("p t (g h) -> p t g h", g=4)
    GS = [("gsb", ti) for ti in range(NTT)]
    for d_ in range(2):
        kb.op("act", lambda e, d_=d_: e.activation(lfs[:, :, d_, :], gview[:, :, 1 + 2 * d_, :], AF.Exp, scale=-1.0),
              reads=GS, writes=[("lfs", d_)])
    kb.op("act", lambda e: e.activation(lfs[:], lfs[:], AF.Ln, bias=1.0), reads=[("lfs", 0), ("lfs", 1)], writes=[("lfs", 0), ("lfs", 1)])
    bg = kb.bank()

    def mmg(e):
        e.matmul(ps[bg][:, 0:72], cstf[:, tU, :], lfs[:, :, 0, :], start=True, stop=True)
        e.matmul(ps[bg][:, 72:144], cstf[:, tL, :], lfs[:, :, 1, :], start=True, stop=True)
        return e.matmul(ps[bg][:, 144:288], cstf[:, tO, :], lfs[:], start=True, stop=True)
    kb.op("pe", mmg, reads=["cstf", ("lfs", 0), ("lfs", 1)], writes=[PS(bg)])
    for d_ in range(2):
        cum = ps[bg][:, 72 * d_:72 * (d_ + 1)].rearrange("p (t h) -> p t h", h=4)
        kb.op("act", lambda e, d_=d_, cum=cum: e.activation(gd[:, 0, :, d_, :], cum, AF.Exp, bias=-LNK),
              reads=[PS(bg)], writes=[("gd", 0, d_)])
        kb.op("dve", lambda e, d_=d_, cum=cum: e.tensor_tensor(tmpc[:, :, d_, :], cum, gview[:, :, 2 * d_, :], ALU.add),
              reads=[PS(bg)] + GS, writes=[("tmpc", d_)])
    kb.op("act", lambda e: e.activation(gd[:, 1, :, :, :], tmpc[:], AF.Exp), reads=[("tmpc", 0), ("tmpc", 1)], writes=[("gd", 1)])
    kb.op("act", lambda e: e.activation(gd[:, 2, :, :, :], ps[bg][:, 144:288].rearrange("p (t d h) -> p t d h", d=2, h=4), AF.Exp, scale=-1.0),
          reads=[PS(bg)], writes=[("gd", 2)])
    GD = [("gd", 0, 0), ("gd", 0, 1), ("gd", 1), ("gd", 2)]
    kb.op("pool", lambda e: e.memset(Cst[:], 0.0), writes=[("Cst", d_, p_) for d_ in range(2) for p_ in range(2)])
    kb.op("pool", lambda e: e.memset(Cbf[:], 0.0), writes=[("Cbf", d_, p_) for d_ in range(2) for p_ in range(2)])

    def ktile(ti):
        if ti < NCT:
            return (lambda h: kTc[:, h, ti * 128:(ti + 1) * 128]), [("kTc", h) for h in range(4)]
        t = ti - NCT
        return (lambda h: kT[:, h, t * 128:(t + 1) * 128]), [("kT", h, t // 4) for h in range(4)]

    cntk = [0]

    def make_ktm(ti):
        i2 = cntk[0] % 2
        cntk[0] += 1
        kf, kkeys = ktile(ti)
        b = kb.bank()

        def tr(e):
            ins = None
            for h in range(4):
                ins = e.transpose(psb[b][:, h * 128:(h + 1) * 128], kf(h), identb[:])
            return ins
        kb.op("pe", tr, reads=kkeys + ["identb"], writes=[PS(b)])
        kb.op("act", lambda e: e.copy(ktm[i2][:], psb[b][:, 0:512].rearrange("p (h d) -> p h d", h=4)),
              reads=[PS(b)], writes=[("ktm", i2)])
        return i2

    cntv = [0, 0]

    def make_vw(ti, d_, eng="pool"):
        i2 = cntv[d_] % 2
        cntv[d_] += 1
        if eng == "act":
            for h in range(4):
                kb.op("act", lambda e, h=h: e.activation(vwb[d_][i2][:, h, :], V1(ti)[:, h, :], AF.Copy, scale=gd[:, 1, ti, d_, h:h + 1]),
                      reads=[V1K(ti), V1O(ti), ("gd", 1)], writes=[("vwb", d_, i2, h)])
        else:
            kb.op("pool", lambda e: e.tensor_tensor(vwb[d_][i2][:], V1(ti),
                                                    gd[:, 1, ti, d_, :].unsqueeze(2).to_broadcast([128, 4, 129]), ALU.mult),
                  reads=[V1K(ti), V1O(ti), ("gd", 1)], writes=[("vwb", d_, i2, h) for h in range(4)])
        return i2

    def state_update(ti, d_, ik, iv, c0_dst=None, cp="pool"):
        for p_ in range(2):
            b = kb.bank()

            def mm(e, b=b, p_=p_):
                ins = None
                for hh in range(2):
                    h = 2 * p_ + hh
                    ins = e.matmul(ps[b][:, hh * 129:(hh + 1) * 129], ktm[ik][:, h, :], vwb[d_][iv][:, h, :], start=True, stop=True)
                return ins
            kb.op("pe", mm, reads=[("ktm", ik)] + [("vwb", d_, iv, h) for h in range(4)], writes=[PS(b)])
            cs = Cst[:, d_, 2 * p_:2 * p_ + 2, :]
            kb.op("dve", lambda e, b=b, cs=cs: e.tensor_tensor(cs, ps[b][:, 0:258].rearrange("p (h e) -> p h e", e=129), cs, ALU.add),
                  reads=[PS(b), ("Cst", d_, p_)], writes=[("Cst", d_, p_)])
            ebb = gd[:, 2, ti, d_, 2 * p_:2 * p_ + 2].unsqueeze(2).to_broadcast([128, 2, 129])
            kb.op("dve", lambda e, cs=cs, ebb=ebb: e.tensor_tensor(cs, cs, ebb, ALU.mult),
                  reads=[("Cst", d_, p_), ("gd", 2)], writes=[("Cst", d_, p_)])
            if cp == "act":
                cpf = lambda e, o_, i_: e.copy(o_, i_)
            else:
                cpf = lambda e, o_, i_: e.tensor_copy(o_, i_)
            if c0_dst is None or d_ == 1:
                kb.op(cp, lambda e, cs=cs, p_=p_: cpf(e, Cbf[:, d_, 2 * p_:2 * p_ + 2, :], cs),
                      reads=[("Cst", d_, p_)], writes=[("Cbf", d_, p_)])
            if c0_dst is not None:
                kb.op(cp, lambda e, cs=cs, p_=p_: cpf(e, C0(c0_dst)[:, 2 * p_:2 * p_ + 2, :], cs),
                      reads=[("Cst", d_, p_)], writes=[C0K(c0_dst, p_)])

    chain = []

    def step_a(ti, d_):
        return make_ktm(ti), make_vw(ti, d_, eng="act")

    def step_b(ti, d_, ik, iv):
        if d_ == 0:
            nxt = ti - NCT + 1
            state_update(ti, 0, ik, iv, c0_dst=nxt if nxt >= 0 else None, cp="act")
        else:
            state_update(ti, 1, ik, iv, cp="act")

    seq = [(ti, 0) for ti in range(0, NTT - 1)] + [(1, 1), (0, 1)]
    held = []

    def mk(i):
        def f():
            if held:
                step_b(*held.pop(0))
            if i < len(seq):
                ti, d_ = seq[i]
                ik, iv = step_a(ti, d_)
                held.append((ti, d_, ik, iv))
        return f
    for i in range(len(seq) + 1):
        chain.append(mk(i))

    def chain_hook():
        if chain:
            chain.pop(0)()

    tm_group(4, 1536, 512, range(NCT, NTT), ev_o, hook=chain_hook)
    tm_group(5, 2048, 512, range(NCT, NTT), ev_q, hook=chain_hook)
    while chain:
        chain_hook()
    if debug:
        kb.dump("gd", gd, GD)
        kb.dump("Cbf", Cbf, [("Cbf", d_, p_) for d_ in range(2) for p_ in range(2)])
    if debug:
        kb.dump("kaT", kaT, [("kaT", ti) for ti in range(NTT)])
        kb.dump("va1", va1, [("va1", ti) for ti in range(NTT)] + [("va1", "ones")])
        kb.dump("gsb", gsb, [("gsb", ti) for ti in range(NTT)])
    if stage <= 2:
        return finish(kb)

    kb.reset(A_END)
    sTm = [[kb.sb("sTm%d_%d" % (d_, i), [128, 4, 128], BF16, key="sTm") for i in range(2)] for d_ in range(2)]
    hsum = [kb.sb("hsum%d" % i, [128, 4, 128], F32, key="hsum") for i in range(2)]
    hsqs = [kb.sb("hsq%d" % i, [128, 4, 128], F32, key="hsq") for i in range(2)]
    ytile = [kb.sb("ytile%d" % i, [128, D], BF16, key="ytile") for i in range(2)]
    qaTb = [kb.sb("qaTb0", [128, 4, 128], BF16, key="qaTb", at=GAINH_AT + 1024),
            kb.sb("qaTb1", [128, 4, 128], BF16, key="qaTb")]
    pTbs = [[kb.sb("pTb%d_%d" % (j_, i), [128, 5, 512], BF16, key="pTb") for i in range(2)] for j_ in range(2)]
    fsm = kb.sb("fsm", [128, 2, 16, 8], F32, at=GAINH_AT)
    if debug:
        kb.op("pool", lambda e: e.memset(fsm[:], 0.0), writes=[("fsm", a_, b_) for a_ in range(2) for b_ in range(16)])
    def attention(T, yt, ykey):
        ti = T + NCT
        i2 = T % 2
        pTb = pTbs[i2]
        b = kb.bank()

        def tr(e):
            ins = None
            for g in range(4):
                ins = e.transpose(psb[b][:, g * 128:(g + 1) * 128], QR(T)[:, g * 128:(g + 1) * 128], identb[:])
            return ins
        kb.op("pe", tr, reads=[QRK(T), "identb"], writes=[PS(b)])
        kb.op("act", lambda e: e.copy(qaTb[i2][:], psb[b][:, 0:512].rearrange("p (g t) -> p g t", g=4)),
              reads=[PS(b)], writes=[("qaTb", i2)])
        yield
        kbs = []
        if T > 0:
            kbs.append((ti - 1, 0))
        if T < NT - 1:
            kbs.append((ti + 1, 1))
        kbs += [(0, None), (1, None), (ti, None)]

        def qk(kbi, kblk, mi, hk):
            b = kb.bank()
            kb.op("pe", lambda e: e.matmul(ps[b][:, :], kaT[64 * hk:64 * hk + 64, kblk * 128:(kblk + 1) * 128],
                                           qaTb[i2][64 * hk:64 * hk + 64, :, :], start=True, stop=True),
                  reads=[("kaT", kblk), ("qaTb", i2)], writes=[PS(b)])
            kb.op("act", lambda e: e.activation(pTb[hk][:, kbi, :], ps[b][:, :], AF.Exp, scale=0.125),
                  reads=[PS(b)], writes=[("pTb", i2, hk, kbi)])
            if mi is not None:
                kb.op("pool", lambda e: e.tensor_tensor(pTb[hk][:, kbi, :], pTb[hk][:, kbi, :], mnegb[:, mi, :], ALU.mult),
                      reads=[("pTb", i2, hk, kbi), "mnegb"], writes=[("pTb", i2, hk, kbi)])
        for kbi, (kblk, mi) in enumerate(kbs):
            for hk in range(2):
                qk(kbi, kblk, mi, hk)
            yield

        def head(hk):
            pb = pTb[hk]
            bo = kb.bank()

            def pv(e):
                ins = None
                for g in range(4):
                    for kbi, (kblk, mi) in enumerate(kbs):
                        ins = e.matmul(ps[bo][:, g * 65:(g + 1) * 65], pb[:, kbi, g * 128:(g + 1) * 128], va1[:, kblk, hk, :],
                                       start=(kbi == 0), stop=(kbi == len(kbs) - 1))
                return ins
            kb.op("pe", pv, reads=[("pTb", i2, hk, kbi) for kbi in range(len(kbs))] + [("va1", kblk) for kblk, _ in kbs] + [("va1", "ones")],
                  writes=[PS(bo)])
            pv3 = ps[bo][:, 0:260].rearrange("p (g e) -> p g e", e=65)
            dn = fsm[:, i2, 8 + hk, 0:4]
            kb.op("dve", lambda e: e.tensor_tensor(dn, pv3[:, :, 64], esink[:, 4 * hk:4 * hk + 4], ALU.add),
                  reads=[PS(bo), "esink"], writes=[("fsm", i2, 8 + hk)])
            kb.op("dve", lambda e: e.reciprocal(dn, dn), reads=[("fsm", i2, 8 + hk)], writes=[("fsm", i2, 8 + hk)])
            kb.op("dve", lambda e: e.tensor_tensor(
                yt[:, 512 + 256 * hk:512 + 256 * (hk + 1)].rearrange("p (g d) -> p g d", d=64), pv3[:, :, 0:64],
                dn.unsqueeze(2).to_broadcast([128, 4, 64]), ALU.mult),
                reads=[PS(bo), ("fsm", i2, 8 + hk)], writes=[ykey + ("a", hk)])
        for hk in range(2):
            head(hk)
            yield

    def mlstm_tile(T, yt, ykey):
        ti = T + NCT
        i2 = T % 2
        ik = make_ktm(ti)
        ivf = make_vw(ti, 0)
        ivb = make_vw(ti, 1)
        yield
        bs = kb.bank()

        def mms(e):
            ins = None
            for h in range(4):
                ins = e.matmul(ps[bs][:, h * 128:(h + 1) * 128], kT[:, h, T * 128:(T + 1) * 128], qT[:, h, T * 128:(T + 1) * 128],
                               start=True, stop=True)
            return ins
        qk_keys = [("kT", h, T // 4) for h in range(4)] + [("qT", h, T // 4) for h in range(4)]
        kb.op("pe", mms, reads=qk_keys, writes=[PS(bs)])
        s3 = ps[bs][:, :].rearrange("p (h t) -> p h t", h=4)
        for d_ in range(2):
            tri = cstf[:, tU if d_ == 0 else tL, :].unsqueeze(1).to_broadcast([128, 4, 128])
            kb.op("dve", lambda e, d_=d_, tri=tri: e.tensor_tensor(sTm[d_][i2][:], s3, tri, ALU.mult),
                  reads=[PS(bs), "cstf"], writes=[("sTm", d_, i2)])
        yield
        hs = hsum[i2]
        while T < NT - 1 and not su_done.get(T + 1, False):
            yield
        bd = kb.bank()

        def mmd(e):
            ins = None
            for d_ in range(2):
                iv = ivf if d_ == 0 else ivb
                for h in range(4):
                    g8 = d_ * 4 + h
                    o = ps[bd][:, 2 * g8:2 * g8 + 2]
                    e.matmul(o, sTm[d_][i2][:, h, :], vwb[d_][iv][:, h, 127:129], start=True, stop=False)
                    c0 = C0(T)[:, h, 127:129] if d_ == 0 else Cbf[:, 1, h, 127:129]
                    ins = e.matmul(o, qT[:, h, T * 128:(T + 1) * 128], c0, start=False, stop=True)
            return ins
        ckeys = [C0K(T, 0), C0K(T, 1), ("Cbf", 1, 0), ("Cbf", 1, 1)]
        vkeys = [("vwb", 0, ivf, h) for h in range(4)] + [("vwb", 1, ivb, h) for h in range(4)]
        kb.op("pe", mmd, reads=[("sTm", 0, i2), ("sTm", 1, i2)] + ckeys + vkeys + qk_keys, writes=[PS(bd)])
        fa = fsm[:, i2, 0, 0:8]
        fb = fsm[:, i2, 1, 0:8]
        den8 = ps[bd][:, 0:16].rearrange("p (g c) -> p g c", c=2)[:, :, 1]
        kb.op("dve", lambda e: e.tensor_tensor(fa, den8, gd[:, 0, ti, :, :].rearrange("p d h -> p (d h)"), ALU.max),
              reads=[PS(bd), ("gd", 0, 0), ("gd", 0, 1)], writes=[("fsm", i2, 0)])
        kb.op("dve", lambda e: e.scalar_tensor_tensor(fb, den8, -1.0, fa, ALU.mult, ALU.max),
              reads=[PS(bd), ("fsm", i2, 0)], writes=[("fsm", i2, 1)])
        kb.op("dve", lambda e: e.reciprocal(fb, fb), reads=[("fsm", i2, 1)], writes=[("fsm", i2, 1)])
        yield
        for idx_, d_ in enumerate((1, 0)):
            iv = ivf if d_ == 0 else ivb
            b = kb.bank()

            def mmp(e, b=b, d_=d_, iv=iv):
                ins = None
                for h in range(4):
                    o = ps[b][:, h * 128:(h + 1) * 128]
                    e.matmul(o, sTm[d_][i2][:, h, :], vwb[d_][iv][:, h, 0:128], start=True, stop=False)
                    c0 = C0(T)[:, h, 0:128] if d_ == 0 else Cbf[:, 1, h, 0:128]
                    ins = e.matmul(o, qT[:, h, T * 128:(T + 1) * 128], c0, start=False, stop=True)
                return ins
            ck = [C0K(T, 0), C0K(T, 1)] if d_ == 0 else [("Cbf", 1, 0), ("Cbf", 1, 1)]
            kb.op("pe", mmp, reads=[("sTm", d_, i2)] + ck + [("vwb", d_, iv, h) for h in range(4)] + qk_keys, writes=[PS(b)])
            p3 = ps[b][:, :].rearrange("p (h e) -> p h e", e=128)
            fbc = fb[:, 4 * d_:4 * d_ + 4].unsqueeze(2).to_broadcast([128, 4, 128])
            if idx_ == 0:
                kb.op("dve", lambda e, p3=p3, fbc=fbc: e.tensor_tensor(hs[:], p3, fbc, ALU.mult),
                      reads=[PS(b), ("fsm", i2, 1)], writes=[("hsum", i2, h) for h in range(4)])
                if T > 0:
                    state_update(ti, 1, ik, ivb, cp="act")
                su_done[T] = True
            else:
                for h in range(4):
                    kb.op("dve", lambda e, p3=p3, h=h, d_=d_: e.scalar_tensor_tensor(hs[:, h, :], p3[:, h, :], fb[:, 4 * d_ + h:4 * d_ + h + 1], hs[:, h, :],
                                                                                 ALU.mult, ALU.add),
                          reads=[PS(b), ("fsm", i2, 1), ("hsum", i2, h)], writes=[("hsum", i2, h)])
            yield
        HK = [("hsum", i2, h) for h in range(4)]
        hsq = hsqs[i2]
        kb.op("act", lambda e: e.activation(hsq[:], hs[:], AF.Square), reads=HK, writes=[("hsq", i2)])
        ssq = fsm[:, i2, 10, 0:4]
        kb.op("dve", lambda e: e.tensor_reduce(ssq, hsq[:], AX.X, ALU.add), reads=[("hsq", i2)], writes=[("fsm", i2, 10)])
        kb.op("act", lambda e: e.activation(ssq, ssq, AF.Ln, scale=1.0 / 128, bias=epsb[:, 0:1]), reads=[("fsm", i2, 10), "epsb"], writes=[("fsm", i2, 10)])
        kb.op("act", lambda e: e.activation(ssq, ssq, AF.Exp, scale=-0.5), reads=[("fsm", i2, 10)], writes=[("fsm", i2, 10)])
        for h in range(4):
            kb.op("dve", lambda e, h=h: e.scalar_tensor_tensor(yt[:, h * 128:(h + 1) * 128], hs[:, h, :], ssq[:, h:h + 1],
                                                               OG(T)[:, h * 128:(h + 1) * 128], ALU.mult, ALU.mult),
                  reads=HK + [("fsm", i2, 10), OGK(T)], writes=[ykey + ("m", h)])
        yield

    su_done = {}

    def finish_tile(T):
        yt = ytile[T % 2]
        ykey = ("ytile", T % 2)
        b = kb.bank()

        def try_(e):
            ins = None
            for c in range(KC):
                ins = e.transpose(psb[b][:, c * 128:(c + 1) * 128], yt[:, c * 128:(c + 1) * 128], identb[:])
            return ins
        kb.op("pe", try_, reads=[ykey + ("m", h_) for h_ in range(4)] + [ykey + ("a", 0), ykey + ("a", 1), "identb"], writes=[PS(b)])
        kb.op("act", lambda e: e.copy(actT[:, :, T * 128:(T + 1) * 128], psb[b][:, :].rearrange("p (c t) -> p c t", c=KC)),
              reads=[PS(b)], writes=AT(T))

    woutb = [kb.sb("woutb0", [128, KC, 512], BF16, key="woutb0", at=OGHI_AT),
             kb.sb("woutb1", [128, KC, 512], BF16, key="woutb1", at=QRHI_AT)]
    wabgH = kb.sb("wabgH", [128, KC, 512], BF16, at=V1HI_AT)
    wabgH2 = kb.sb("wabgH2", [128, KC, 512], BF16, at=C0HI_AT)

    def prefetch_p3():
        kb.dma("pool", wabgH[:], wada_v[:, :, 2048:2560], writes=["wabgH"])
        kb.dma("pool", wabgH2[:], wada_v[:, :, 2560:3072], writes=["wabgH2"])
        for half in range(2):
            kb.dma("pool", woutb[half][:], wout_v[:, :, half * 512:(half + 1) * 512], writes=[("woutb%d" % half, 0)])

    pending = list(range(NT - 1, -1, -1))
    active = []
    since = 99
    NFLY = 2 if F_INTER else 1
    while pending or active:
        if pending and len(active) < NFLY and since >= 5:
            T = pending.pop(0)
            active.append([T, [attention(T, ytile[T % 2], ("ytile", T % 2)), mlstm_tile(T, ytile[T % 2], ("ytile", T % 2))]])
            since = 0
        since += 1
        for ent in list(active):
            T, gens = ent
            for g_ in list(gens):
                try:
                    next(g_)
                except StopIteration:
                    gens.remove(g_)
            if not gens:
                finish_tile(T)
                active.remove(ent)
                if T == NLO:
                    prefetch_p3()
    if debug:
        kb.dump("yT", actT, [k_ for t in range(NT) for k_ in AT(t)])
    if stage <= 3:
        return finish(kb)

    kb.reset(P_END)
    x1 = kb.sb("x1", [128, NT, D], F32)
    X1_END = kb.mark()
    assert X1_END <= V1HI_AT
    kb.reset(P1_KEEP)
    gabc = kb.sb("gabc", [128, D], F32)
    gfbc = kb.sb("gfbc", [128, D], F32)
    fnb = kb.sb("fnb", [128, D], F32)
    P3_KEEP = kb.mark()
    wabg = [kb.sb("wabg%d" % i, [128, KC, 1024], BF16, key="wabg") for i in range(2)]
    bgt = kb.sb("bgt", [128, 2, 1024], F32)
    xb3 = [kb.sb("xb3_%d" % i, [128, D], F32, key="xb3") for i in range(2)]
    NX3 = 2
    xn3 = [kb.sb("xn3_%d" % i, [128, D], BF16, key="xn3") for i in range(NX3)]
    sqj3 = kb.sb("sqj3", [128, D], BF16, at=X1_END)
    tmpo = [kb.sb("tmpo%d" % i, [128, 512], F32, key="tmpo") for i in range(2)]
    kb.dma("sp", bgt[:], badag_d, writes=["bgt"])

    def g_half(buf, bkey, gi, half, dst, dkey, c0=None):
        b = kb.bank()
        c0 = half * 512 if c0 is None else c0

        def mm(e):
            ins = None
            for k in range(KC):
                ins = e.matmul(ps[b][:, :], silrep[:, k, :], buf[:, k, c0:c0 + 512], start=(k == 0), stop=(k == KC - 1))
            return ins
        kb.op("pe", mm, reads=["silrep", bkey], writes=[PS(b)])
        kb.op("dve", lambda e: e.tensor_tensor(dst[:, half * 512:(half + 1) * 512], ps[b][:, :],
                                               bgt[:, gi, half * 512:(half + 1) * 512], ALU.add),
              reads=[PS(b), "bgt"], writes=[(dkey, half)])

    def g_piece(gi, bi, dst, dkey):
        for half in range(2):
            g_half(wabg[bi], ("wabg", bi), gi, half, dst, dkey)

    g_half(wabgH, "wabgH", 0, 0, gabc, "gabc", c0=0)
    g_half(wabgH2, "wabgH2", 0, 1, gabc, "gabc", c0=0)
    kb.dma("pool", wabg[1][:], wada_v[:, :, 3072:4096], writes=[("wabg", 1)])
    kb.dma("pool", wabg[0][:], wada_v[:, :, 4096:5120], writes=[("wabg", 0)])
    kb.dma("sp", fnb[:], fnorm_d, writes=["fnb"])
    if debug:
        kb.dump("gabc", gabc, [("gabc", 0), ("gabc", 1)])

    def norm_stats(T, si):
        src = x1[:, T, :]
        kb.op("act", lambda e: e.activation(sqj3[:], src, AF.Square, accum_out=stat[:, 0, si:si + 1]),
              reads=[("x1", T, 0), ("x1", T, 1)], writes=["sqj3", ("stat0", si)])
        rstd_ops(si)

    def norm_sb(T, si):
        i2 = T % NX3
        src = x1[:, T, :]
        kb.op("dve", lambda e: e.tensor_scalar(xn3[i2][:], src, stat[:, 2, si:si + 1], None, ALU.mult),
              reads=[("x1", T, 0), ("x1", T, 1), ("stat2", si)], writes=[("xn3", i2)])
        tr_evac(xn3[i2], ("xn3", i2), 4, 5, actT, T * 128, ("actT", T), NDV=4)

    def outproj(T):
        i2 = T % 2
        kb.dma("sp", xb3[i2][:], x_v[T], writes=[("xb3", i2)])
        for half in range(2):
            b = kb.bank()
            j4 = half

            def mm(e, b=b, half=half):
                ins = None
                for k in range(KC):
                    ins = e.matmul(ps[b][:, :], actT[:, k, T * 128:(T + 1) * 128], woutb[half][:, k, :],
                                   start=(k == 0), stop=(k == KC - 1))
                return ins
            kb.op("pe", mm, reads=AT(T) + [("woutb%d" % half, 0)], writes=[PS(b)])
            kb.op("dve", lambda e, b=b, half=half, j4=j4: e.tensor_tensor(tmpo[j4][:], ps[b][:, :], gabc[:, half * 512:(half + 1) * 512], ALU.mult),
                  reads=[PS(b), ("gabc", half)], writes=[("tmpo", j4)])
            kb.op("pool", lambda e, half=half, j4=j4: e.tensor_tensor(x1[:, T, half * 512:(half + 1) * 512], tmpo[j4][:],
                                                                      xb3[i2][:, half * 512:(half + 1) * 512], ALU.add),
                  reads=[("tmpo", j4), ("xb3", i2)], writes=[("x1", T, half)])

    def extras_a():
        adaln_fm(3, wabg[1], ("wabg", 1))

    def extras_b():
        adaln_fm(4, wabg[0], ("wabg", 0))
        kb.dma("pool", wabg[1][:], wada_v[:, :, 5120:6144], writes=[("wabg", 1)])
        kb.op("dve", lambda e: e.scalar_tensor_tensor(scsh[:, 4, :], modfm[:, 4, :, 0], 1.0, nf, ALU.add, ALU.mult),
              reads=[("modfm", 4), "vecs"], writes=[("scsh", 4)])
        kb.op("dve", lambda e: e.tensor_copy(scsh[:, 5, :], modfm[:, 3, :, 0]), reads=[("modfm", 3)], writes=[("scsh", 5)])

    def extras_c():
        g_piece(1, 1, gfbc, "gfbc")

    LAG = 6
    for T in range(NT):
        outproj(T)
        norm_stats(T, NTT + T)
        if T == 3:
            extras_a()
        if T == 6:
            extras_b()
        if T >= LAG:
            norm_sb(T - LAG, NTT + T - LAG)
    extras_c()
    for T in range(NT - LAG, NT):
        norm_sb(T, NTT + T)
    flush_ev()
    if debug:
        kb.dump("x1", x1, [("x1", T, h_) for T in range(NT) for h_ in range(2)])
        kb.dump("h2T", actT, [k_ for t in range(NT) for k_ in AT(t)])
    if stage <= 4:
        return finish(kb)

    GROUPS = [(0, 5), (5, 10), (10, 14), (14, 18), (18, 22)]
    NRING = 8
    kb.reset(X1_END + 2048)
    gT = kb.sb("gT", [128, 5, L], BF16)
    wub = [kb.sb("wub%d" % i, [128, KC, 256], BF16, key="wub") for i in range(2)]
    assert kb.mark() <= P1_KEEP
    kb.reset(P3_KEEP)
    wdb = kb.sb("wdb", [128, NRING, D], BF16)
    asb = [kb.sb("asb%d" % i, [128, L + 2], F32, key="asb") for i in range(2)]
    facc = [kb.sb("facc%d" % i, [128, 512], F32, key="facc") for i in range(2)]
    fth = [kb.sb("fth%d" % i, [128, 512], F32, key="fth") for i in range(2)]
    sqj4 = kb.sb("sqj4", [128, D], BF16)
    for i in range(2):
        kb.op("pool", lambda e, i=i: e.memset(asb[i][:, 0:1], 0.0), writes=[("asb", i, "h0")])
        kb.op("pool", lambda e, i=i: e.memset(asb[i][:, L + 1:L + 2], 0.0), writes=[("asb", i, "h1")])
    out_toks = []
    cntf = [0]

    def load_w(cgi):
        i2 = cgi % 2
        kb.dma("pool", wub[i2][:, :, 0:128], wup_v[:, :, cgi * 128:(cgi + 1) * 128], writes=[("wub", i2, 0)])
        kb.dma("pool", wub[i2][:, :, 128:256], wup_v[:, :, DFF + cgi * 128:DFF + (cgi + 1) * 128], writes=[("wub", i2, 1)])
        kb.dma("pool", wdb[:, cgi % NRING, :], wdn_d[cgi * 128:(cgi + 1) * 128, :], writes=[("wdb", cgi % NRING)])
        kb.op("pool", lambda e: e.tensor_tensor(wdb[:, cgi % NRING, :], wdb[:, cgi % NRING, :], gfbc[:], ALU.mult),
              reads=[("wdb", cgi % NRING), ("gfbc", 0), ("gfbc", 1)], writes=[("wdb", cgi % NRING)])

    def ffn_cg(cgi, cl):
        i2 = cgi % 2
        a = asb[i2]
        for tg in range(4):
            b = kb.bank()

            def mm(e, b=b, tg=tg):
                ins = None
                for k in range(KC):
                    ins = e.matmul(ps[b][:, :], wub[i2][:, k, 0:128], actT[:, k, tg * 512:(tg + 1) * 512], start=(k == 0), stop=(k == KC - 1))
                return ins
            kb.op("pe", mm, reads=[("wub", i2, 0)] + [k_ for t in range(tg * 4, tg * 4 + 4) for k_ in AT(t)], writes=[PS(b)])
            kb.op("act", lambda e, b=b, tg=tg: e.copy(a[:, 1 + tg * 512:1 + (tg + 1) * 512], ps[b][:, :]), reads=[PS(b)], writes=[("asb", i2, tg)])
        akeys = [("asb", i2, x_) for x_ in (0, 1, 2, 3, "h0", "h1")]

        def piece(tg):
            p0 = tg * 512
            j2 = cntf[0] % 2
            cntf[0] += 1
            fa, ft = facc[j2], fth[j2]
            bv = kb.bank()

            def mmv(e):
                ins = None
                for k in range(KC):
                    ins = e.matmul(ps[bv][:, :], wub[i2][:, k, 128:256], actT[:, k, p0:p0 + 512], start=(k == 0), stop=(k == KC - 1))
                return ins
            kb.op("pe", mmv, reads=[("wub", i2, 1)] + [k_ for t in range(tg * 4, tg * 4 + 4) for k_ in AT(t)], writes=[PS(bv)])
            kb.op("act", lambda e: e.activation(fa[:], a[:, 1 + p0:1 + p0 + 512], AF.Identity, bias=ffcw[:, cgi, 3:4], scale=ffcw[:, cgi, 1:2]),
                  reads=akeys + ["ffcw"], writes=[("facc", j2)])
            kb.op("dve", lambda e: e.scalar_tensor_tensor(fa[:], a[:, p0:p0 + 512], ffcw[:, cgi, 0:1], fa[:], ALU.mult, ALU.add),
                  reads=akeys + ["ffcw", ("facc", j2)], writes=[("facc", j2)])
            kb.op("dve", lambda e: e.scalar_tensor_tensor(fa[:], a[:, 2 + p0:2 + p0 + 512], ffcw[:, cgi, 2:3], fa[:], ALU.mult, ALU.add),
                  reads=akeys + ["ffcw", ("facc", j2)], writes=[("facc", j2)])
            kb.op("act", lambda e: e.activation(ft[:], fa[:], AF.Gelu_apprx_tanh), reads=[("facc", j2)], writes=[("fth", j2)])
            kb.op("dve", lambda e: e.tensor_tensor(gT[:, cl, p0:p0 + 512], ft[:], ps[bv][:, :], ALU.mult),
                  reads=[("fth", j2), PS(bv)], writes=[("gT", cl, tg)])
        for tg in range(4):
            piece(tg)

    def down(gi, c0, c1, T, last):
        i2 = T % 2
        for half in range(2):
            b = kb.bank()

            def mm(e, b=b, half=half):
                ins = None
                for cgi in range(c0, c1):
                    ins = e.matmul(ps[b][:, :], gT[:, cgi - c0, T * 128:(T + 1) * 128], wdb[:, cgi % NRING, half * 512:(half + 1) * 512],
                                   start=(cgi == c0), stop=(cgi == c1 - 1))
                return ins
            kb.op("pe", mm, reads=[("gT", cgi - c0, T // 4) for cgi in range(c0, c1)] + [("wdb", cgi % NRING) for cgi in range(c0, c1)],
                  writes=[PS(b)])
            kb.op("dve", lambda e, b=b, half=half: e.tensor_tensor(x1[:, T, half * 512:(half + 1) * 512], ps[b][:, :],
                                                                   x1[:, T, half * 512:(half + 1) * 512], ALU.add),
                  reads=[PS(b), ("x1", T, half)], writes=[("x1", T, half)])
        if last:
            si = NTT + NT + T
            kb.op("act", lambda e: e.activation(sqj4[:], x1[:, T, :], AF.Square, accum_out=stat[:, 0, si:si + 1]),
                  reads=[("x1", T, 0), ("x1", T, 1)], writes=["sqj4", ("stat0", si)])
            rstd_ops(si)
            def fin(T=T, si=si):
                kb.op("dve", lambda e: e.scalar_tensor_tensor(x1[:, T, :], x1[:, T, :], stat[:, 2, si:si + 1], fnb[:], ALU.mult, ALU.mult),
                      reads=[("x1", T, 0), ("x1", T, 1), ("stat2", si), "fnb"], writes=[("x1", T, 0), ("x1", T, 1)])
                out_toks.append(kb.dma("sp", out_v[T], x1[:, T, :], reads=[("x1", T, 0), ("x1", T, 1)]))
            pend_fin.append(fin)
            while len(pend_fin) > 1:
                pend_fin.pop(0)()

    pend_fin = []

    load_w(0)
    for gi, (c0, c1) in enumerate(GROUPS):
        for cgi in range(c0, c1):
            if cgi + 1 < NCG:
                load_w(cgi + 1)
            ffn_cg(cgi, cgi - c0)
        for T in range(NT):
            down(gi, c0, c1, T, gi == len(GROUPS) - 1)
    while pend_fin:
        pend_fin.pop(0)()
    kb.dumps.extend(out_toks)
    return finish(kb)


def finish(kb):
    toks = list(kb.dumps)
    kb.wait_all("sp", toks)
    return kb.build()


def _fm(v):
    return np.ascontiguousarray(np.asarray(v, np.float32).reshape(-1, 128).T)


def _bcast(v, n=128):
    v = np.asarray(v, np.float32)
    return np.ascontiguousarray(np.broadcast_to(v[None], (n,) + v.shape))


def _rope_table():
    tok = np.arange(L)
    row = (tok // 64).astype(np.float64)
    col = (tok % 64).astype(np.float64)
    inv = 10000.0 ** (-np.arange(16, dtype=np.float64) / 16)
    tab = np.zeros((L, 2, 64), np.float64)
    for d in range(64):
        axis, half, p = d // 32, (d % 32) // 16, d % 16
        ang = (row if axis == 0 else col) * inv[p]
        tab[:, 0, d] = np.cos(ang)
        tab[:, 1, d] = -np.sin(ang) if half == 0 else np.sin(ang)
    return np.ascontiguousarray(tab.reshape(NT, 128, 2, 64).transpose(1, 0, 2, 3)).astype(np.float32)


def _consts():
    s = np.arange(128)[:, None]
    t = np.arange(128)[None, :]
    c = np.zeros((128, 6, 128), np.float32)
    c[:, 0] = (s == t)
    c[:, 1] = (s <= t)
    c[:, 2] = (s >= t)
    c[:, 3] = 1.0
    c[:, 4] = np.where(s >= t, 0.0, NEG)
    c[:, 5] = np.where(s <= t, 0.0, NEG)
    return c


_QH = np.concatenate([2064 + 64 * h + np.arange(64) for h in (0, 4, 1, 5, 2, 6, 3, 7)])
_PERM = np.concatenate([np.arange(0, 2048), _QH, np.arange(2576, 2832), np.arange(2048, 2064)])


def host_maps(x, c, ctx, c_ctx, w_ada, b_ada, norm_mix, norm_ffn, w_in, gate_b, qk_conv_w, qk_conv_b,
              mlstm_norm, attn_sink, w_out, w_up, ffn_conv_w, ffn_conv_b, w_down, final_norm, cores=range(8)):
    f = lambda a: np.ascontiguousarray(np.asarray(a, np.float32))
    b_ada0 = f(b_ada)[0]
    shared = {
        "bada_fm": _fm(b_ada0),
        "bada_g": _bcast(np.stack([b_ada0[2048:3072], b_ada0[5120:6144]])),
        "qkcw": np.ascontiguousarray(np.stack([_fm(f(qk_conv_w)[0, 0]), _fm(f(qk_conv_w)[0, 1]), _fm(f(qk_conv_w)[0, 2]),
                                               _fm(f(qk_conv_b)[0])], axis=-1)),
        "ffcw": np.ascontiguousarray(np.stack([_fm(f(ffn_conv_w)[0, 0]), _fm(f(ffn_conv_w)[0, 1]), _fm(f(ffn_conv_w)[0, 2]),
                                               _fm(f(ffn_conv_b)[0])], axis=-1)),
        "gate_b": _bcast(f(gate_b)[0]),
        "gain": _bcast(f(mlstm_norm)[0]),
        "sink": _bcast(f(attn_sink)[0]),
        "fnorm": _bcast(f(final_norm)),
        "rope": _rope_table(),
        "consts": _consts(),
        "w_ada": f(w_ada)[0],
        "w_in": np.ascontiguousarray(f(w_in)[0][:, _PERM]),
        "w_out": f(w_out)[0],
        "w_up": f(w_up)[0],
        "w_down": f(w_down)[0],
    }
    maps = []
    xs, cs, ctxs = f(x), f(c), f(ctx)
    for b in cores:
        m = dict(shared)
        m["x"] = xs[b]
        m["ctx"] = ctxs[b]
        m["vecs"] = np.ascontiguousarray(np.stack([_fm(cs[b]), _fm(f(c_ctx)), _fm(f(norm_mix)[0]), _fm(f(norm_ffn)[0])], axis=1))
        maps.append(m)
    return maps


_NC_CACHE = {}


def kernel(**inputs):
    if "nc" not in _NC_CACHE:
        _NC_CACHE["nc"] = build()
    nc = _NC_CACHE["nc"]
    maps = host_maps(**inputs)
    res = run_bass_kernel_spmd(nc, maps, core_ids=list(range(8)))
    return np.stack([np.asarray(r["out"], np.float32) for r in res.results], axis=0)
```
